# Optimizing a Trainium2 kernel written in Bass

```python
import jax, jax.numpy as jnp
from jax import lax
import numpy as np

D_MODEL = 2048
BATCH = 16
SEQ = 2048
DEPTH = 4

CHUNK = 64
MIX_WIDTH = D_MODEL
CONV_CH = MIX_WIDTH // 2
CONV_K = 31
RET_HEADS = 4
RET_DK = (MIX_WIDTH // 2) // RET_HEADS
RET_DV = (MIX_WIDTH // 2) // RET_HEADS
RET_QK_W = RET_HEADS * RET_DK
RET_V_W = RET_HEADS * RET_DV
EVEN_IN_W = 2 * CONV_CH + 2 * RET_QK_W + 2 * RET_V_W
EVEN_OUT_IN = CONV_CH + RET_V_W
GM_WIDTH = D_MODEL
GM_BLOCK = 128
GM_GROUPS = 4
D_FF = ((8 * D_MODEL // 3 + 255) // 256) * 256
FFN_K = 3
ROPE_THETA = 10000.0
RMS_EPS = 1e-6
LN_EPS = 1e-5
N_EVEN = (DEPTH + 1) // 2
N_ODD = DEPTH // 2
RESID_SCALE = (2 * DEPTH) ** -0.5

kernel_name = 'hybrid_conv_retention_gmlp_encoder'


def rms_norm(x, g):
    xf = x.astype(jnp.float32)
    y = xf * lax.rsqrt(jnp.mean(xf * xf, axis=-1, keepdims=True) + RMS_EPS)
    return (y * g.astype(jnp.float32)).astype(x.dtype)


def layer_norm(x, g, b):
    xf = x.astype(jnp.float32)
    mu = jnp.mean(xf, axis=-1, keepdims=True)
    var = jnp.mean(jnp.square(xf - mu), axis=-1, keepdims=True)
    y = (xf - mu) * lax.rsqrt(var + LN_EPS)
    return (y * g.astype(jnp.float32) + b.astype(jnp.float32)).astype(x.dtype)


def causal_dwconv(x, w, b):
    k = w.shape[0]
    y = lax.conv_general_dilated(
        x, w[:, None, :].astype(x.dtype), window_strides=(1,), padding=[(k - 1, 0)],
        dimension_numbers=('NWC', 'WIO', 'NWC'), feature_group_count=x.shape[-1])
    return y + b.astype(x.dtype)


def apply_rotary(x):
    s, dh = x.shape[1], x.shape[-1]
    inv = ROPE_THETA ** (-jnp.arange(0, dh, 2, dtype=jnp.float32) / dh)
    ang = jnp.arange(s, dtype=jnp.float32)[:, None] * inv[None, :]
    cos = jnp.cos(ang)[None, :, None, :].astype(x.dtype)
    sin = jnp.sin(ang)[None, :, None, :].astype(x.dtype)
    x1, x2 = x[..., : dh // 2], x[..., dh // 2:]
    return jnp.concatenate([x1 * cos - x2 * sin, x1 * sin + x2 * cos], axis=-1)


def conformer_conv(a, b, w_dw, b_dw, ln_g, ln_b):
    u = a * jax.nn.sigmoid(b)
    u = causal_dwconv(u, w_dw, b_dw)
    return jax.nn.silu(layer_norm(u, ln_g, ln_b))


def retention(q, k, v, gate, ln_g):
    bsz, s, _ = q.shape
    L = CHUNK
    nc = s // L
    H = RET_HEADS
    dt = q.dtype
    q = apply_rotary(q.reshape(bsz, s, H, RET_DK))
    k = apply_rotary(k.reshape(bsz, s, H, RET_DK)) * (RET_DK ** -0.5)
    v = v.reshape(bsz, s, H, RET_DV)
    q = q.reshape(bsz, nc, L, H, RET_DK).transpose(0, 3, 1, 2, 4)
    k = k.reshape(bsz, nc, L, H, RET_DK).transpose(0, 3, 1, 2, 4)
    v = v.reshape(bsz, nc, L, H, RET_DV).transpose(0, 3, 1, 2, 4)

    log_gamma = jnp.log1p(-jnp.exp2(-5.0 - jnp.arange(H, dtype=jnp.float32)))
    pos = jnp.arange(L, dtype=jnp.float32)
    dist = jnp.abs(pos[:, None] - pos[None, :])
    d_intra = jnp.exp(dist[None] * log_gamma[:, None, None]).astype(dt)
    xi = jnp.exp((pos + 1.0)[None] * log_gamma[:, None]).astype(dt)
    zeta = jnp.exp((L - 1.0 - pos)[None] * log_gamma[:, None]).astype(dt)
    g_chunk = jnp.exp(L * log_gamma)

    scores = jnp.einsum('bhcnd,bhcmd->bhcnm', q, k) * d_intra[:, None]
    o_intra = jnp.einsum('bhcnm,bhcme->bhcne', scores, v)

    q_s = jnp.moveaxis(q * xi[:, None, :, None], 2, 0)
    k_s = jnp.moveaxis(k * zeta[:, None, :, None], 2, 0)
    v_s = jnp.moveaxis(v, 2, 0)

    def step(state, inp):
        qc, kc, vc = inp
        cross = jnp.einsum('bhnd,bhde->bhne', qc.astype(jnp.float32), state).astype(dt)
        state = g_chunk[None, :, None, None] * state + jnp.einsum(
            'bhmd,bhme->bhde', kc, vc).astype(jnp.float32)
        return state, cross

    state0 = jnp.zeros((bsz, H, RET_DK, RET_DV), jnp.float32)
    _, o_cross = lax.scan(step, state0, (q_s, k_s, v_s))
    o = o_intra + jnp.moveaxis(o_cross, 0, 2)
    o = o.transpose(0, 2, 3, 1, 4).reshape(bsz, s, H, RET_DV)

    of = o.astype(jnp.float32)
    mu = jnp.mean(of, axis=-1, keepdims=True)
    var = jnp.mean(jnp.square(of - mu), axis=-1, keepdims=True)
    on = ((of - mu) * lax.rsqrt(var + LN_EPS)).reshape(bsz, s, RET_V_W)
    on = (on * ln_g.astype(jnp.float32)).astype(dt)
    return on * jax.nn.silu(gate)


def even_mixer(h, w_in, w_dw, b_dw, cln_g, cln_b, rln_g, w_out):
    p = h @ w_in
    offs = np.cumsum([CONV_CH, CONV_CH, RET_QK_W, RET_QK_W, RET_V_W]).tolist()
    ga, gb, q, k, v, gate = jnp.split(p, offs, axis=-1)
    a_out = conformer_conv(ga, gb, w_dw, b_dw, cln_g, cln_b)
    b_out = retention(q, k, v, gate, rln_g)
    return jnp.concatenate([a_out, b_out], axis=-1) @ w_out


def spatial_gate(v, ws, bs):
    bsz, s, w = v.shape
    nb = s // GM_BLOCK
    vb = v.reshape(bsz, nb, GM_BLOCK, GM_GROUPS, w // GM_GROUPS)
    cidx = jnp.arange(GM_BLOCK) // CHUNK
    mask = cidx[None, :] <= cidx[:, None]
    wm = jnp.where(mask[None], ws, jnp.zeros_like(ws)).astype(v.dtype)
    out = jnp.einsum('gij,bnjgc->bnigc', wm, vb) + bs.T.astype(v.dtype)[None, None, :, :, None]
    return out.reshape(bsz, s, w)


def odd_mixer(h, w_in, ln_g, ln_b, ws, bs, w_out):
    p = jax.nn.gelu(h @ w_in, approximate=False)
    u, v = jnp.split(p, 2, axis=-1)
    v = spatial_gate(layer_norm(v, ln_g, ln_b), ws, bs)
    return (u * v) @ w_out


def conv_ffn(h, w_up, dw_w, dw_b, w_down):
    up = causal_dwconv(h @ w_up, dw_w, dw_b)
    a, b = jnp.split(up, 2, axis=-1)
    return (jax.nn.silu(a) * b) @ w_down


def setup_inputs(seed: int = 0) -> dict:
    key = jax.random.key(seed)
    ks = jax.random.split(key, 24)
    f32 = jnp.float32
    nrm = lambda k, shape, scale: jax.random.normal(k, shape, f32) * scale
    gain = lambda k, shape: 1.0 + 0.02 * jax.random.normal(k, shape, f32)
    return {
        'x': jax.random.normal(ks[0], (BATCH, SEQ, D_MODEL), f32),
        'mix_norm_g': gain(ks[1], (DEPTH, D_MODEL)),
        'ffn_norm_g': gain(ks[2], (DEPTH, D_MODEL)),
        'final_norm_g': gain(ks[3], (D_MODEL,)),
        'ev_w_in': nrm(ks[4], (N_EVEN, D_MODEL, EVEN_IN_W), D_MODEL ** -0.5),
        'ev_conv_dw_w': nrm(ks[5], (N_EVEN, CONV_K, CONV_CH), CONV_K ** -0.5),
        'ev_conv_dw_b': nrm(ks[6], (N_EVEN, CONV_CH), 0.01),
        'ev_conv_ln_g': gain(ks[7], (N_EVEN, CONV_CH)),
        'ev_conv_ln_b': nrm(ks[8], (N_EVEN, CONV_CH), 0.01),
        'ev_ret_ln_g': gain(ks[9], (N_EVEN, RET_V_W)),
        'ev_w_out': nrm(ks[10], (N_EVEN, EVEN_OUT_IN, D_MODEL), EVEN_OUT_IN ** -0.5 * RESID_SCALE),
        'od_w_in': nrm(ks[11], (N_ODD, D_MODEL, 2 * GM_WIDTH), D_MODEL ** -0.5),
        'od_gm_ln_g': gain(ks[12], (N_ODD, GM_WIDTH)),
        'od_gm_ln_b': nrm(ks[13], (N_ODD, GM_WIDTH), 0.01),
        'od_gm_ws': nrm(ks[14], (N_ODD, GM_GROUPS, GM_BLOCK, GM_BLOCK), GM_BLOCK ** -0.5),
        'od_gm_bs': gain(ks[15], (N_ODD, GM_GROUPS, GM_BLOCK)),
        'od_w_out': nrm(ks[16], (N_ODD, GM_WIDTH, D_MODEL), GM_WIDTH ** -0.5 * RESID_SCALE),
        'ffn_w_up': nrm(ks[17], (DEPTH, D_MODEL, 2 * D_FF), D_MODEL ** -0.5),
        'ffn_dw_w': nrm(ks[18], (DEPTH, FFN_K, 2 * D_FF), FFN_K ** -0.5),
        'ffn_dw_b': nrm(ks[19], (DEPTH, 2 * D_FF), 0.01),
        'ffn_w_down': nrm(ks[20], (DEPTH, D_FF, D_MODEL), D_FF ** -0.5 * RESID_SCALE),
    }


def reference(x, mix_norm_g, ffn_norm_g, final_norm_g, ev_w_in, ev_conv_dw_w, ev_conv_dw_b,
              ev_conv_ln_g, ev_conv_ln_b, ev_ret_ln_g, ev_w_out, od_w_in, od_gm_ln_g,
              od_gm_ln_b, od_gm_ws, od_gm_bs, od_w_out, ffn_w_up, ffn_dw_w, ffn_dw_b,
              ffn_w_down):
    for i in range(DEPTH):
        j = i // 2
        h = rms_norm(x, mix_norm_g[i])
        if i % 2 == 0:
            y = even_mixer(h, ev_w_in[j], ev_conv_dw_w[j], ev_conv_dw_b[j], ev_conv_ln_g[j],
                           ev_conv_ln_b[j], ev_ret_ln_g[j], ev_w_out[j])
        else:
            y = odd_mixer(h, od_w_in[j], od_gm_ln_g[j], od_gm_ln_b[j], od_gm_ws[j],
                          od_gm_bs[j], od_w_out[j])
        x = x + y
        h = rms_norm(x, ffn_norm_g[i])
        x = x + conv_ffn(h, ffn_w_up[i], ffn_dw_w[i], ffn_dw_b[i], ffn_w_down[i])
    return rms_norm(x, final_norm_g)
```

```python
import numpy as np
from contextlib import ExitStack
import concourse.bass as bass
import concourse.mybir as mybir
from concourse.bass_utils import run_bass_kernel_spmd

F32 = mybir.dt.float32
BF16 = mybir.dt.bfloat16
AF = mybir.ActivationFunctionType
ALU = mybir.AluOpType

D = 2048
KC = 16
T = 512
NB = T // 128
SEQ = 2048
TPS = SEQ // T
NSEQ = 2
NCORE = 8
DFF = 5632
NSLOT = 2
RMS_EPS = 1e-6
LN_EPS = 1e-5
ARENA_W = 69 * 256


class Buf:
    __slots__ = ("last_write", "readers")

    def __init__(self, readers=None):
        self.last_write = None
        self.readers = list(readers) if readers else []


class Prog:
    ENGS = ("pe", "act", "dve", "pool", "sp")

    def __init__(self, nc, stack):
        self.nc = nc
        self.stack = stack
        self.ops = {e: [] for e in self.ENGS}
        self.sem = {}
        self.sems = []
        self.semval = []
        for e in self.ENGS:
            self.sem[e] = self.new_sem("s_" + e)
        self.seen = {e: {} for e in self.ENGS}
        self.n_ops = 0
        self.n_wait = 0

    def new_sem(self, name):
        h = self.stack.enter_context(self.nc.semaphore(name))
        self.sems.append(h)
        self.semval.append(0)
        return len(self.sems) - 1

    def barrier(self):
        return [(self.sem[e], self.semval[self.sem[e]]) for e in ("pe", "act", "dve") if self.semval[self.sem[e]] > 0]

    def op(self, eng, fn, reads=(), writes=(), deps=(), sem=None, inc=1, mark=True):
        need = {}

        def add(tok):
            if tok is None:
                return
            s, v = tok
            if need.get(s, 0) < v:
                need[s] = v

        for b in reads:
            add(b.last_write)
        for b in writes:
            add(b.last_write)
            for t in b.readers:
                add(t)
        for t in deps:
            add(t)
        waits = []
        seen = self.seen[eng]
        own = self.sem[eng]
        for s, v in need.items():
            if s == own and eng == "pe":
                continue
            if seen.get(s, 0) >= v:
                continue
            seen[s] = v
            waits.append((s, v))
        if mark:
            si = own if sem is None else sem
            self.semval[si] += inc
            tok = (si, self.semval[si])
            inc_si = si
        else:
            tok = (own, self.semval[own] + 1)
            inc_si = None
        sems = self.sems
        self.n_ops += 1
        self.n_wait += len(waits)

        def emit(e, waits=waits, fn=fn, inc_si=inc_si, inc=inc):
            for s, v in waits:
                e.wait_ge(sems[s], v)
            ins = fn(e)
            if inc_si is not None:
                ins.then_inc(sems[inc_si], inc)

        self.ops[eng].append(emit)
        for b in reads:
            b.readers.append(tok)
            if len(b.readers) > 48:
                mx = {}
                for t in b.readers:
                    if mx.get(t[0], 0) < t[1]:
                        mx[t[0]] = t[1]
                b.readers = list(mx.items())
        for b in writes:
            b.last_write = tok
            b.readers = []
        return tok

    def wait_all(self, eng, toks):
        sems = self.sems
        toks = [t for t in toks if t is not None]

        def emit(e):
            for s, v in toks:
                e.wait_ge(sems[s], v)

        self.ops[eng].append(emit)

    def finish(self):
        with self.nc.Block() as block:
            @block.tensor
            def _(e):
                for f in self.ops["pe"]:
                    f(e)

            @block.scalar
            def _(e):
                for f in self.ops["act"]:
                    f(e)

            @block.vector
            def _(e):
                for f in self.ops["dve"]:
                    f(e)

            @block.gpsimd
            def _(e):
                for f in self.ops["pool"]:
                    f(e)

            @block.sync
            def _(e):
                for f in self.ops["sp"]:
                    f(e)


CST_SPEC = [
    ("mixg", 4 * 16), ("ffng", 4 * 16), ("fing", 16),
    ("fdw", 4 * 3 * 88), ("fdb", 4 * 88),
    ("cdw", 2 * 31 * 8), ("cdb", 2 * 8), ("clg", 2 * 8), ("clb", 2 * 8), ("rlg", 2 * 8),
    ("glg", 2 * 16), ("glb", 2 * 16),
    ("zeta", 4), ("maskT", 4 * 128), ("xi", 4 * 128), ("bsbc", 2 * 4 * 128),
    ("wmask", 128), ("ident", 128),
]
CST_OFF = {}
_o = 0
for _n, _w in CST_SPEC:
    CST_OFF[_n] = _o
    _o += _w
NCST = _o


def layer_pieces(li):
    out = []
    if li % 2 == 0:
        out += [("glu", gp) for gp in range(4)]
        for hp in range(2):
            out += [("q", hp), ("k", hp), ("v", hp), ("gate", hp)]
        out += [("mout", m) for m in range(4)]
    else:
        out += [("gu", m) for m in range(4)]
        out += [("gv", m) for m in range(4)]
        out += [("mout", m) for m in range(4)]
    out += [("up", i) for i in range(22)]
    out += [("down", m) for m in range(16)]
    return out


PIECE_BASE = []
_b = 0
for _li in range(4):
    PIECE_BASE.append(_b)
    _b += len(layer_pieces(_li))
NPIECE = _b


def build_program(n_tiles=NSEQ * TPS, layers=(0, 1, 2, 3), do_ffn=True, final_norm=True):
    nc = bass.Bass("TRN2", target_bir_lowering=False)
    xT = nc.dram_tensor("xT", [D, NSEQ * SEQ], F32, kind="ExternalInput").ap()
    wst = nc.dram_tensor("wstream", [NPIECE, 128, 8192], F32, kind="ExternalInput").ap()
    cst_d = nc.dram_tensor("cst", [128, NCST], F32, kind="ExternalInput").ap()
    wsT_d = nc.dram_tensor("wsT", [128, 1024], F32, kind="ExternalInput").ap()
    rope_d = nc.dram_tensor("rope", [2, 128, SEQ], F32, kind="ExternalInput").ap()
    outT = nc.dram_tensor("outT", [D, NSEQ * SEQ], F32, kind="ExternalOutput").ap()
    xT_v = xT.rearrange("(k p) t -> p k t", p=128)
    outT_v = outT.rearrange("(k p) t -> p k t", p=128)

    with ExitStack() as st:
        P = Prog(nc, st)
        sb = lambda name, shape, dt: st.enter_context(nc.sbuf_tensor(name, shape, dt))
        x_sb = sb("x_sb", [128, KC, T], F32)
        h_sb = sb("h_sb", [128, KC, T], BF16)
        slots = [sb(f"wslot{i}", [128, 8192], BF16) for i in range(NSLOT)]
        cst = sb("cst_sb", [128, NCST], F32)
        cos_sb = sb("cos_sb", [128, T], F32)
        sin_sb = sb("sin_sb", [128, T], F32)
        S_f32 = sb("S_f32", [128, 2, 4, 512], F32)
        S_bf = sb("S_bf", [128, 4, 512], BF16)
        uh = sb("uh", [128, 2, 8, 30], F32)
        fh = sb("fh", [128, 4, 88, 2], F32)
        ident = sb("ident", [128, 128], BF16)
        ones = sb("ones", [128, 128], BF16)
        wmT = sb("wmT", [128, 2, 4, 128], BF16)
        rowsum = sb("rowsum", [128, 2, 4, 128], F32)
        arena = sb("arena", [128, ARENA_W], F32)
        ps = [st.enter_context(nc.psum_tensor(f"ps{i}", [128, 512], F32)) for i in range(8)]

        xb = [Buf() for _ in range(KC)]
        hb = [Buf() for _ in range(KC)]
        pb = [Buf() for _ in range(8)]
        slot_b = [Buf() for _ in range(NSLOT)]
        slot_sem = [P.new_sem(f"slot{i}") for i in range(NSLOT)]
        cst_b = Buf()
        cst_sem = P.new_sem("cst")
        cos_b, sin_b = Buf(), Buf()
        cos_sem, sin_sem = P.new_sem("cos"), P.new_sem("sin")
        x_sems = [P.new_sem(f"xld{i}") for i in range(4)]
        o_sem = P.new_sem("ost")
        Sf_b = [[Buf() for _ in range(4)] for _ in range(2)]
        Sb_b = [Buf() for _ in range(4)]
        uh_b = [[Buf() for _ in range(8)] for _ in range(2)]
        fh_b = [Buf() for _ in range(4)]
        misc_b = Buf()

        def C(name, idx=0, n=1):
            o = CST_OFF[name] + idx
            return cst[:, o:o + n]

        extra_bar = []
        class Arena:
            def __init__(self, start=0):
                self.off = start
                self.bar = P.barrier() + list(extra_bar)

            def f32(self, n):
                a = arena[:, self.off:self.off + n]
                self.off += n
                assert self.off <= ARENA_W, self.off
                return a, Buf(self.bar)

            def bf16(self, n):
                w = (n + 1) // 2
                a = arena[:, self.off:self.off + w].bitcast(BF16)
                self.off += w
                assert self.off <= ARENA_W, self.off
                return a, Buf(self.bar)

        pass_pieces = []
        for li in layers:
            kinds = layer_pieces(li)
            for i, kd in enumerate(kinds):
                if not do_ffn and kd[0] in ("up", "down"):
                    continue
                pass_pieces.append((PIECE_BASE[li] + i, 5632 if kd[0] == "down" else 8192, (li,) + kd))
        total_pieces = n_tiles * len(pass_pieces)
        wstate = {"load": 0, "use": 0}

        def w_load():
            p = wstate["load"]
            if p >= total_pieces:
                return
            hidx, n, _ = pass_pieces[p % len(pass_pieces)]
            s = p % NSLOT
            P.op("pool", lambda e, s=s, hidx=hidx, n=n: e.dma_start(out=slots[s][:, 0:n], in_=wst[hidx, :, 0:n]),
                 writes=[slot_b[s]], sem=slot_sem[s], inc=16)
            wstate["load"] += 1

        def w_acquire(expect):
            p = wstate["use"]
            _, _, kd = pass_pieces[p % len(pass_pieces)]
            assert kd == expect, (kd, expect)
            s = p % NSLOT
            return slots[s], slot_b[s]

        def w_release():
            wstate["use"] += 1
            w_load()

        bank_rr = {"i": 0}

        def nbank():
            b = bank_rr["i"]
            bank_rr["i"] = (b + 1) % 6
            return b

        def mm(out, lhsT, rhs, start, stop, reads, bank, mark=None):
            if mark is None:
                mark = stop
            return P.op("pe", lambda e: e.matmul(out, lhsT, rhs, start=start, stop=stop),
                        reads=reads, writes=[pb[bank]], mark=mark)

        def act(out, in_, func, reads, writes, bias=None, scale=None):
            kw = {}
            if bias is not None:
                kw["bias"] = bias
            if scale is not None:
                kw["scale"] = scale
            return P.op("act", lambda e: e.activation(out, in_, func, **kw), reads=reads, writes=writes)

        def tt(out, in0, in1, op, reads, writes):
            return P.op("dve", lambda e: e.tensor_tensor(out, in0, in1, op), reads=reads, writes=writes)

        def stt(out, in0, scalar, in1, op0, op1, reads, writes):
            return P.op("dve", lambda e: e.scalar_tensor_tensor(out, in0, scalar, in1, op0, op1), reads=reads, writes=writes)

        def ts(out, in0, s1, s2, op0, op1, reads, writes):
            return P.op("dve", lambda e: e.tensor_scalar(out, in0, s1, s2, op0, op1), reads=reads, writes=writes)

        def recip(out, in_, reads, writes):
            return P.op("dve", lambda e: e.reciprocal(out, in_), reads=reads, writes=writes)

        P.op("sp", lambda e: e.dma_start(out=cst[:], in_=cst_d), writes=[cst_b], sem=cst_sem, inc=16)
        for _ in range(NSLOT):
            w_load()
        ar = Arena()
        wtmp, wtmp_b = ar.f32(1024)
        wt_sem = P.new_sem("wtmp")
        P.op("sp", lambda e: e.dma_start(out=wtmp, in_=wsT_d), writes=[wtmp_b], sem=wt_sem, inc=16)
        P.op("dve", lambda e: e.memset(ones[:], 1.0), writes=[misc_b])
        P.op("act", lambda e: e.copy(out=ident[:], in_=C("ident", 0, 128)), reads=[cst_b], writes=[misc_b])
        tt(wmT[:].rearrange("p j g i -> p (j g) i"), wtmp.rearrange("p (a i) -> p a i", a=8),
           C("wmask", 0, 128).unsqueeze(1).to_broadcast([128, 8, 128]), ALU.mult, [wtmp_b, cst_b], [misc_b])
        for j in range(2):
            mm(ps[j][:], ones[:], wmT[:, j].rearrange("p g i -> p (g i)"), True, True, [misc_b], j)
            P.op("act", lambda e, j=j: e.copy(out=rowsum[:, j].rearrange("p g i -> p (g i)"), in_=ps[j][:]),
                 reads=[], writes=[pb[j], misc_b])

        def rmsnorm(gname, gidx, in_place=False):
            ar = Arena()
            sq = [ar.bf16(T) for _ in range(2)]
            sd, sd_b = ar.f32(T)
            rstd, rstd_b = ar.f32(T)
            for kc in range(KC):
                a, ab = sq[kc % 2]
                act(a, x_sb[:, kc, :], AF.Square, [xb[kc]], [ab])
                mm(ps[6][:], ones[:], a, kc == 0, kc == KC - 1, [ab, misc_b], 6, mark=True)
            act(sd, ps[6][:], AF.Sqrt, [], [pb[6], sd_b], bias=RMS_EPS, scale=1.0 / D)
            recip(rstd, sd, [sd_b], [rstd_b])
            if in_place:
                stg, stg_b = ar.f32(KC * T)
            for kc in range(KC):
                if in_place:
                    stt(stg[:, kc * T:(kc + 1) * T], x_sb[:, kc, :], C(gname, gidx * 16 + kc), rstd, ALU.mult, ALU.mult,
                        [rstd_b, cst_b, xb[kc]], [stg_b])
                else:
                    stt(h_sb[:, kc, :], x_sb[:, kc, :], C(gname, gidx * 16 + kc), rstd, ALU.mult, ALU.mult,
                        [rstd_b, cst_b, xb[kc]], [hb[kc]])
            if in_place:
                return stg, stg_b

        def proj_fm(sv, sbuf, m, bank, rhs_of, rbufs_of, nk=KC):
            for kc in range(nk):
                mm(ps[bank][:], sv[:, kc, m * 128:(m + 1) * 128], rhs_of(kc), kc == 0, kc == nk - 1,
                   [sbuf] + rbufs_of(kc), bank)

        def out_proj(li, cat, catb):
            for mp in range(4):
                slot, sbuf = w_acquire((li, "mout", mp))
                sv = slot[:].rearrange("p (k c) -> p k c", k=KC)
                for m in range(4):
                    mo = mp * 4 + m
                    bank = nbank()
                    proj_fm(sv, sbuf, m, bank, lambda kc: cat[:, kc, :], lambda kc: [catb[kc]])
                    tt(x_sb[:, mo, :], x_sb[:, mo, :], ps[bank][:], ALU.add, [], [xb[mo], pb[bank]])
                w_release()

        def even_mixer(li):
            j = li // 2
            ar = Arena()
            cat_f, _ = ar.bf16(KC * T)
            cat = cat_f.rearrange("p (k t) -> p k t", k=KC)
            catb = [Buf(ar.bar) for _ in range(KC)]
            cat_end = ar.off
            c_f, _ = ar.f32(8 * T)
            c_sb = c_f.rearrange("p (k t) -> p k t", k=8)
            cb = [Buf(ar.bar) for _ in range(8)]
            ub_f, _ = ar.bf16(8 * (32 + T))
            u_bf = ub_f.rearrange("p (k t) -> p k t", k=8)
            uib = [Buf(ar.bar) for _ in range(8)]
            tb = [ar.bf16(T) for _ in range(4)]
            ln_off = ar.off
            sig = [ar.f32(T) for _ in range(2)]
            dg = []
            for _ in range(2):
                d_f, d_b = ar.bf16(31 * 128)
                dg.append((d_f.rearrange("p (k i) -> p k i", k=31), d_b))
            def glu_chunk(sv, sbuf, gp, cc):
                c = 2 * gp + cc
                bA, bB = nbank(), nbank()
                proj_fm(sv, sbuf, cc, bA, lambda kc: h_sb[:, kc, :], lambda kc: [hb[kc]])
                proj_fm(sv, sbuf, 2 + cc, bB, lambda kc: h_sb[:, kc, :], lambda kc: [hb[kc]])
                sg, sgb = sig[c % 2]
                act(sg, ps[bB][:], AF.Sigmoid, [], [pb[bB], sgb])
                P.op("act", lambda e, c=c: e.copy(out=u_bf[:, c, 0:30], in_=uh[:, j, c, :]),
                     reads=[uh_b[j][c]], writes=[uib[c]])
                tt(u_bf[:, c, 30:30 + T], ps[bA][:], sg, ALU.mult, [sgb], [pb[bA], uib[c]])
                P.op("act", lambda e, c=c: e.copy(out=uh[:, j, c, :], in_=u_bf[:, c, T:T + 30]),
                     reads=[uib[c]], writes=[uh_b[j][c]])
                dgc, dgb = dg[c % 2]
                o = CST_OFF["cdw"] + j * 31 * 8 + c
                wk = cst[:, o:o + 31 * 8:8]
                tt(dgc, ident[:].unsqueeze(1).to_broadcast([128, 31, 128]),
                   wk.unsqueeze(2).to_broadcast([128, 31, 128]), ALU.mult, [misc_b, cst_b], [dgb])

            def conv_chunk(c):
                dgc, dgb = dg[c % 2]
                bank = nbank()
                for k in range(31):
                    mm(ps[bank][:], dgc[:, k, :], u_bf[:, c, k:k + T], k == 0, k == 30, [dgb, uib[c]], bank)
                a, ab = tb[(2 * c) % 4]
                q, qb = tb[(2 * c + 1) % 4]
                bia = C("cdb", j * 8 + c)
                act(c_sb[:, c, :], ps[bank][:], AF.Identity, [cst_b], [pb[bank], cb[c]], bias=bia, scale=1.0)
                act(a, ps[bank][:], AF.Identity, [cst_b], [pb[bank], ab], bias=bia, scale=1.0)
                act(q, ps[bank][:], AF.Square, [cst_b], [pb[bank], qb], bias=bia, scale=1.0)
                mm(ps[6][:], ones[:], a, c == 0, c == 7, [ab, misc_b], 6, mark=True)
                mm(ps[7][:], ones[:], q, c == 0, c == 7, [qb, misc_b], 7, mark=True)

            pend = None
            for gp in range(4):
                slot, sbuf = w_acquire((li, "glu", gp))
                sv = slot[:].rearrange("p (k c) -> p k c", k=KC)
                for cc in range(2):
                    glu_chunk(sv, sbuf, gp, cc)
                    if pend is not None:
                        conv_chunk(pend)
                    pend = 2 * gp + cc
                w_release()
            conv_chunk(pend)
            ar = Arena(ln_off)
            mean, mean_b = ar.f32(T)
            msq, msq_b = ar.f32(T)
            rstd, rstd_b = ar.f32(T)
            act(mean, ps[6][:], AF.Identity, [], [pb[6], mean_b], scale=1.0 / 1024)
            tt(msq, mean, mean, ALU.mult, [mean_b], [msq_b])
            stt(msq, ps[7][:], 1.0 / 1024, msq, ALU.mult, ALU.subtract, [], [pb[7], msq_b])
            act(msq, msq, AF.Sqrt, [], [msq_b], bias=LN_EPS, scale=1.0)
            recip(rstd, msq, [msq_b], [rstd_b])
            for c in range(8):
                tt(c_sb[:, c, :], c_sb[:, c, :], mean, ALU.subtract, [mean_b], [cb[c]])
                tt(c_sb[:, c, :], c_sb[:, c, :], rstd, ALU.mult, [rstd_b], [cb[c]])
                act(cat[:, c, :], c_sb[:, c, :], AF.Silu, [cb[c], cst_b], [catb[c]],
                    bias=C("clb", j * 8 + c), scale=C("clg", j * 8 + c))
            ar = Arena(cat_end)
            qT_f, qT_b = ar.bf16(4 * T)
            kT_f, kT_b = ar.bf16(4 * T)
            qx_f, qx_b = ar.bf16(4 * T)
            qT = qT_f.rearrange("p (k t) -> p k t", k=4)
            kT = kT_f.rearrange("p (k t) -> p k t", k=4)
            qx = qx_f.rearrange("p (k t) -> p k t", k=4)
            v_f, v_b = ar.bf16(NB * 512)
            sg_f, sg_b = ar.bf16(NB * 512)
            kz_f, kz_b = ar.bf16(NB * 512)
            v_sb = v_f.rearrange("p (b e) -> p b e", b=NB)
            sg_sb = sg_f.rearrange("p (b e) -> p b e", b=NB)
            kz = kz_f.rearrange("p (b e) -> p b e", b=NB)
            t1, t1_b = ar.f32(T)
            t2, t2_b = ar.f32(T)
            sc = [ar.bf16(128) for _ in range(8)]
            on = [ar.f32(256) for _ in range(4)]
            bo = [ar.bf16(256) for _ in range(4)]
            stats = [ar.f32(8) for _ in range(4)]
            mv = [ar.f32(4) for _ in range(4)]
            for hp in range(2):
                for nm, dst, dst_b in (("q", qT, qT_b), ("k", kT, kT_b)):
                    slot, sbuf = w_acquire((li, nm, hp))
                    sv = slot[:].rearrange("p (k c) -> p k c", k=KC)
                    for hl in range(2):
                        bA, bB = nbank(), nbank()
                        proj_fm(sv, sbuf, 2 * hl, bA, lambda kc: h_sb[:, kc, :], lambda kc: [hb[kc]])
                        proj_fm(sv, sbuf, 2 * hl + 1, bB, lambda kc: h_sb[:, kc, :], lambda kc: [hb[kc]])
                        tt(t1, ps[bA][:], cos_sb[:], ALU.mult, [cos_b], [pb[bA], t1_b])
                        tt(t2, ps[bB][:], sin_sb[:], ALU.mult, [sin_b], [pb[bB], t2_b])
                        tt(dst[:, 2 * hl, :], t1, t2, ALU.subtract, [t1_b, t2_b], [dst_b])
                        tt(t1, ps[bA][:], sin_sb[:], ALU.mult, [sin_b], [pb[bA], t1_b])
                        tt(t2, ps[bB][:], cos_sb[:], ALU.mult, [cos_b], [pb[bB], t2_b])
                        tt(dst[:, 2 * hl + 1, :], t1, t2, ALU.add, [t1_b, t2_b], [dst_b])
                    w_release()
                    if nm == "q":
                        for hl in range(2):
                            hh = 2 * hp + hl
                            for dc in range(2):
                                tt(qx[:, 2 * hl + dc, :].rearrange("p (b n) -> p b n", b=NB),
                                   qT[:, 2 * hl + dc, :].rearrange("p (b n) -> p b n", b=NB),
                                   C("xi", hh * 128, 128).unsqueeze(1).to_broadcast([128, NB, 128]),
                                   ALU.mult, [qT_b, cst_b], [qx_b])
                    else:
                        for hl in range(2):
                            hh = 2 * hp + hl
                            for b in range(NB):
                                bank = nbank()
                                pv = ps[bank][:, 0:128].bitcast(BF16)
                                for dc in range(2):
                                    P.op("pe", lambda e, pv=pv, dc=dc, hl=hl, b=b: e.transpose(
                                        pv[:, dc * 128:(dc + 1) * 128], kT[:, 2 * hl + dc, b * 128:(b + 1) * 128], ident[:]),
                                        reads=[kT_b, misc_b], writes=[pb[bank]])
                                act(kz[:, b, hl * 256:(hl + 1) * 256], pv, AF.Identity, [cst_b], [pb[bank], kz_b],
                                    scale=C("zeta", hh))
                for nm, dst, dst_b, fn in (("v", v_sb, v_b, AF.Identity), ("gate", sg_sb, sg_b, AF.Silu)):
                    slot, sbuf = w_acquire((li, nm, hp))
                    sv = slot[:].rearrange("p (k c) -> p k c", k=KC)
                    for b in range(NB):
                        bank = nbank()
                        for kc in range(KC):
                            mm(ps[bank][:], h_sb[:, kc, b * 128:(b + 1) * 128], sv[:, kc, :], kc == 0, kc == KC - 1,
                               [sbuf, hb[kc]], bank)
                        act(dst[:, b, :], ps[bank][:], fn, [], [pb[bank], dst_b])
                    w_release()
                for step in range(2 * NB):
                    b, hl = step // 2, step % 2
                    blk = slice(b * 128, (b + 1) * 128)
                    bsc = 0 if step < 4 else 4
                    cs = slice((step % 4) * 128, (step % 4 + 1) * 128)
                    for dc in range(2):
                        mm(ps[bsc][:, cs], kT[:, 2 * hl + dc, blk], qT[:, 2 * hl + dc, blk], dc == 0, dc == 1,
                           [kT_b, qT_b], bsc)
                for step in range(2 * NB):
                    hl = step % 2
                    hh = 2 * hp + hl
                    bsc = 0 if step < 4 else 4
                    cs = slice((step % 4) * 128, (step % 4 + 1) * 128)
                    scs, scb = sc[step]
                    tt(scs, ps[bsc][:, cs], C("maskT", hh * 128, 128), ALU.mult, [cst_b], [pb[bsc], scb])
                pend_T = None
                for b in range(NB):
                    blk = slice(b * 128, (b + 1) * 128)
                    for hl in range(2):
                        hh = 2 * hp + hl
                        step = 2 * b + hl
                        bo_, bst, bT = 4 * hl + 1, 4 * hl + 2, 4 * hl + 3
                        scs, scb = sc[step]
                        for dc in range(2):
                            mm(ps[bo_][:, 0:256], qx[:, 2 * hl + dc, blk], S_bf[:, hh, dc * 256:(dc + 1) * 256],
                               dc == 0, False, [qx_b, Sb_b[hh]], bo_, mark=False)
                        mm(ps[bo_][:, 0:256], scs, v_sb[:, b, hl * 256:(hl + 1) * 256], False, True, [scb, v_b], bo_)
                        for dc in range(2):
                            mm(ps[bst][:, dc * 256:(dc + 1) * 256], kz[:, b, hl * 256 + dc * 128: hl * 256 + (dc + 1) * 128],
                               v_sb[:, b, hl * 256:(hl + 1) * 256], True, True, [kz_b, v_b], bst, mark=(dc == 1))
                        if pend_T is not None:
                            pend_T()
                        gk = float((1.0 - 2.0 ** (-5 - hh)) ** 128)
                        stt(S_f32[:, j, hh, :], S_f32[:, j, hh, :], gk, ps[bst][:], ALU.mult, ALU.add,
                            [], [Sf_b[j][hh], pb[bst]])
                        P.op("act", lambda e, hh=hh: e.copy(out=S_bf[:, hh, :], in_=S_f32[:, j, hh, :]),
                             reads=[Sf_b[j][hh]], writes=[Sb_b[hh]])
                        sta, stb = stats[step % 4]
                        mva, mvb = mv[step % 4]
                        ona, onb = on[step % 4]
                        boa, bob = bo[step % 4]
                        P.op("dve", lambda e, sta=sta, bo_=bo_: e.bn_stats(sta[:, 0:6], ps[bo_][:, 0:256]),
                             reads=[], writes=[pb[bo_], stb])
                        P.op("dve", lambda e, sta=sta, mva=mva: e.bn_aggr(mva[:, 0:2], sta[:, 0:6]), reads=[stb], writes=[mvb])
                        act(mva[:, 2:3], mva[:, 1:2], AF.Sqrt, [], [mvb], bias=LN_EPS, scale=1.0)
                        recip(mva[:, 3:4], mva[:, 2:3], [], [mvb])
                        ts(ona, ps[bo_][:, 0:256], mva[:, 0:1], mva[:, 3:4], ALU.subtract, ALU.mult, [mvb], [pb[bo_], onb])
                        tt(boa, ona, sg_sb[:, b, hl * 256:(hl + 1) * 256], ALU.mult, [onb, sg_b], [bob])

                        def do_T(boa=boa, bob=bob, bT=bT, hh=hh, blk=blk):
                            pv = ps[bT][:, 0:128].bitcast(BF16)
                            for ec in range(2):
                                P.op("pe", lambda e, pv=pv, ec=ec, boa=boa: e.transpose(
                                    pv[:, ec * 128:(ec + 1) * 128], boa[:, ec * 128:(ec + 1) * 128], ident[:]),
                                    reads=[bob, misc_b], writes=[pb[bT]])
                            for ec in range(2):
                                cidx = 8 + 2 * hh + ec
                                act(cat[:, cidx, blk], pv[:, ec * 128:(ec + 1) * 128], AF.Identity, [cst_b],
                                    [pb[bT], catb[cidx]], scale=C("rlg", j * 8 + 2 * hh + ec))
                        pend_T = do_T
                pend_T()
            out_proj(li, cat, catb)

        def odd_mixer(li):
            j = li // 2
            ar = Arena()
            uT_f, _ = ar.bf16(KC * T)
            uT = uT_f.rearrange("p (k t) -> p k t", k=KC)
            ub = [Buf(ar.bar) for _ in range(KC)]
            vt_f, _ = ar.f32(NB * 2048)
            v_tok = vt_f.rearrange("p (b f) -> p b f", b=NB)
            vtb = [Buf(ar.bar) for _ in range(NB)]
            bias_f, bias_b = ar.f32(16 * 128)
            bias_t = bias_f.rearrange("p (f i) -> p f i", f=16)
            tmp = [ar.f32(512) for _ in range(4)]
            stats, stats_b = ar.f32(24)
            mv, mv_b = ar.f32(4)
            for fc in range(16):
                g = fc // 4
                stt(bias_t[:, fc, :], rowsum[:, j, g, :], C("glb", j * 16 + fc), C("bsbc", (j * 4 + g) * 128, 128),
                    ALU.mult, ALU.add, [misc_b, cst_b], [bias_b])
            for mp in range(4):
                slot, sbuf = w_acquire((li, "gu", mp))
                sv = slot[:].rearrange("p (k c) -> p k c", k=KC)
                for m in range(4):
                    mo = mp * 4 + m
                    bank = nbank()
                    proj_fm(sv, sbuf, m, bank, lambda kc: h_sb[:, kc, :], lambda kc: [hb[kc]])
                    act(uT[:, mo, :], ps[bank][:], AF.Gelu, [], [pb[bank], ub[mo]])
                w_release()
            for cg in range(4):
                slot, sbuf = w_acquire((li, "gv", cg))
                sv = slot[:].rearrange("p (k c) -> p k c", k=KC)
                for b in range(NB):
                    bank = nbank()
                    for kc in range(KC):
                        mm(ps[bank][:], h_sb[:, kc, b * 128:(b + 1) * 128], sv[:, kc, :], kc == 0, kc == KC - 1,
                           [sbuf, hb[kc]], bank)
                    act(v_tok[:, b, cg * 512:(cg + 1) * 512], ps[bank][:], AF.Gelu, [], [pb[bank], vtb[b]])
                w_release()
            vn = h_sb[:].rearrange("p k t -> p (k t)").rearrange("p (b f) -> p b f", b=NB)
            for b in range(NB):
                for cg in range(4):
                    P.op("dve", lambda e, b=b, cg=cg: e.bn_stats(stats[:, cg * 6:(cg + 1) * 6], v_tok[:, b, cg * 512:(cg + 1) * 512]),
                         reads=[vtb[b]], writes=[stats_b])
                P.op("dve", lambda e: e.bn_aggr(mv[:, 0:2], stats[:, 0:24]), reads=[stats_b], writes=[mv_b])
                act(mv[:, 2:3], mv[:, 1:2], AF.Sqrt, [], [mv_b], bias=LN_EPS, scale=1.0)
                recip(mv[:, 3:4], mv[:, 2:3], [], [mv_b])
                ts(vn[:, b, :], v_tok[:, b, :], mv[:, 0:1], mv[:, 3:4], ALU.subtract, ALU.mult, [mv_b, vtb[b]],
                   [hb[4 * b + i] for i in range(4)])
            for b in range(NB):
                blk = slice(b * 128, (b + 1) * 128)
                for g in range(4):
                    bank = nbank()
                    for fl in range(4):
                        fc = 4 * g + fl
                        mm(ps[bank][:, fl * 128:(fl + 1) * 128], vn[:, b, fc * 128:(fc + 1) * 128], wmT[:, j, g, :], True, True,
                           [hb[4 * b + fc // 4], misc_b], bank, mark=(fl == 3))
                    tm, tmb = tmp[(b * 4 + g) % 4]
                    for fl in range(4):
                        fc = 4 * g + fl
                        act(tm[:, fl * 128:(fl + 1) * 128], ps[bank][:, fl * 128:(fl + 1) * 128], AF.Identity, [cst_b],
                            [pb[bank], tmb], scale=C("glg", j * 16 + fc))
                    tt(tm, tm, bias_f[:, 4 * g * 128:(4 * g + 4) * 128], ALU.add, [bias_b], [tmb])
                    tt(uT[:, 4 * g:4 * g + 4, blk], tm.rearrange("p (f i) -> p f i", f=4), uT[:, 4 * g:4 * g + 4, blk],
                       ALU.mult, [tmb], [ub[4 * g + i] for i in range(4)])
            out_proj(li, uT, ub)

        def ffn(li):
            ar = Arena()
            g_f, _ = ar.bf16(44 * T)
            g_sb = g_f.rearrange("p (k t) -> p k t", k=44)
            gb = [Buf(ar.bar) for _ in range(44)]
            ya = [ar.f32(T) for _ in range(2)]
            yb = [ar.f32(T) for _ in range(2)]
            sa = [ar.f32(T) for _ in range(2)]
            corr_f, corr_b = ar.f32(176)
            corr = corr_f.rearrange("p (c t) -> p c t", t=2)
            t88, t88_b = ar.f32(88)
            w0 = C("fdw", (li * 3 + 0) * 88, 88)
            w1 = C("fdw", (li * 3 + 1) * 88, 88)
            tt(corr[:, :, 1], w0, fh[:, li, :, 1], ALU.mult, [cst_b, fh_b[li]], [corr_b])
            tt(t88, w0, fh[:, li, :, 0], ALU.mult, [cst_b, fh_b[li]], [t88_b])
            tt(corr[:, :, 0], w1, fh[:, li, :, 1], ALU.mult, [cst_b, fh_b[li]], [corr_b])
            tt(corr[:, :, 0], corr[:, :, 0], t88, ALU.add, [t88_b], [corr_b])
            for i in range(22):
                slot, sbuf = w_acquire((li, "up", i))
                sv = slot[:].rearrange("p (k c) -> p k c", k=KC)
                for cc in range(2):
                    ci = 2 * i + cc
                    bA, bB = nbank(), nbank()
                    proj_fm(sv, sbuf, cc, bA, lambda kc: h_sb[:, kc, :], lambda kc: [hb[kc]])
                    proj_fm(sv, sbuf, 2 + cc, bB, lambda kc: h_sb[:, kc, :], lambda kc: [hb[kc]])
                    ys = (ya[ci % 2], yb[ci % 2])
                    for (bank, ch, (y, y_b)) in ((bA, ci, ys[0]), (bB, 44 + ci, ys[1])):
                        act(y, ps[bank][:], AF.Identity, [cst_b], [pb[bank], y_b],
                            bias=C("fdb", li * 88 + ch), scale=C("fdw", (li * 3 + 2) * 88 + ch))
                        P.op("act", lambda e, bank=bank, ch=ch: e.copy(out=fh[:, li, ch, :], in_=ps[bank][:, T - 2:T]),
                             reads=[corr_b], writes=[pb[bank], fh_b[li]])
                        stt(y[:, 1:T], ps[bank][:, 0:T - 1], C("fdw", (li * 3 + 1) * 88 + ch), y[:, 1:T], ALU.mult, ALU.add,
                            [], [pb[bank], y_b])
                        stt(y[:, 2:T], ps[bank][:, 0:T - 2], C("fdw", (li * 3 + 0) * 88 + ch), y[:, 2:T], ALU.mult, ALU.add,
                            [], [pb[bank], y_b])
                        tt(y[:, 0:2], y[:, 0:2], corr[:, ch, :], ALU.add, [corr_b], [y_b])
                    s, s_b = sa[ci % 2]
                    act(s, ys[0][0], AF.Silu, [ys[0][1]], [s_b])
                    tt(g_sb[:, ci, :], s, ys[1][0], ALU.mult, [s_b, ys[1][1]], [gb[ci]])
                w_release()
            for m in range(16):
                slot, sbuf = w_acquire((li, "down", m))
                bank = nbank()
                for kc in range(44):
                    mm(ps[bank][:], slot[:, kc * 128:(kc + 1) * 128], g_sb[:, kc, :], kc == 0, kc == 43, [sbuf, gb[kc]], bank)
                tt(x_sb[:, m, :], x_sb[:, m, :], ps[bank][:], ALU.add, [], [xb[m], pb[bank]])
                w_release()

        last_store = None
        for ti in range(n_tiles):
            seq_first = (ti % TPS == 0)
            tok0 = ti * T
            pos0 = (ti % TPS) * T
            for qd in range(4):
                P.op("sp", lambda e, tok0=tok0, qd=qd: e.dma_start(out=x_sb[:, 4 * qd:4 * qd + 4, :],
                                                                 in_=xT_v[:, 4 * qd:4 * qd + 4, tok0:tok0 + T]),
                     writes=xb[4 * qd:4 * qd + 4], sem=x_sems[qd], inc=16)
            P.op("sp", lambda e, pos0=pos0: e.dma_start(out=cos_sb[:], in_=rope_d[0, :, pos0:pos0 + T]),
                 writes=[cos_b], sem=cos_sem, inc=16)
            P.op("sp", lambda e, pos0=pos0: e.dma_start(out=sin_sb[:], in_=rope_d[1, :, pos0:pos0 + T]),
                 writes=[sin_b], sem=sin_sem, inc=16)
            if seq_first:
                P.op("dve", lambda e: e.memset(S_f32[:], 0.0), writes=[b for r in Sf_b for b in r])
                P.op("dve", lambda e: e.memset(S_bf[:], 0.0), writes=Sb_b)
                P.op("dve", lambda e: e.memset(uh[:], 0.0), writes=[b for r in uh_b for b in r])
                P.op("dve", lambda e: e.memset(fh[:], 0.0), writes=fh_b)
            for li in layers:
                rmsnorm("mixg", li)
                if li % 2 == 0:
                    if not seq_first:
                        for hh in range(4):
                            P.op("act", lambda e, hh=hh, li=li: e.copy(out=S_bf[:, hh, :], in_=S_f32[:, li // 2, hh, :]),
                                 reads=[Sf_b[li // 2][hh]], writes=[Sb_b[hh]])
                    else:
                        P.op("dve", lambda e: e.memset(S_bf[:], 0.0), writes=Sb_b)
                    even_mixer(li)
                else:
                    odd_mixer(li)
                if do_ffn:
                    rmsnorm("ffng", li)
                    ffn(li)
            if final_norm:
                stg, stg_b = rmsnorm("fing", 0, in_place=True)
                last_store = P.op("sp", lambda e, tok0=tok0, stg=stg: e.dma_start(
                    out=outT_v[:, :, tok0:tok0 + T], in_=stg.rearrange("p (k t) -> p k t", k=KC)),
                    reads=[stg_b], sem=o_sem, inc=16)
                extra_bar[:] = [last_store]
            else:
                last_store = P.op("sp", lambda e, tok0=tok0: e.dma_start(out=outT_v[:, :, tok0:tok0 + T], in_=x_sb[:]),
                                  reads=xb, sem=o_sem, inc=16)
        P.wait_all("sp", [last_store, cos_b.last_write, sin_b.last_write, cst_b.last_write])
        P.finish()
        build_program.stats = (P.n_ops, P.n_wait)
    return nc


def _pm(v, c):
    v = np.asarray(v, np.float32)
    lead = v.shape[:-1]
    return np.moveaxis(v.reshape(lead + (c, 128)), -1, 0)


def _col_piece(W, cols):
    blk = W[:, cols]
    return blk.reshape(16, 128, 512).transpose(1, 0, 2).reshape(128, 8192)


def _consts():
    H = 4
    hh = np.arange(H, dtype=np.float64)
    gamma = 1.0 - 2.0 ** (-5.0 - hh)
    pos = np.arange(128, dtype=np.float64)
    n = pos[None, :]
    m = pos[:, None]
    cm = (m // 64)
    cn = (n // 64)
    maskT = np.zeros((128, H, 128))
    for h in range(H):
        same = gamma[h] ** np.abs(n - m)
        caus = gamma[h] ** (n - m)
        maskT[:, h, :] = np.where(cm == cn, same, np.where(cn > cm, caus, 0.0)) / 16.0
    zeta = np.stack([gamma[h] ** (127.0 - pos) / 16.0 for h in range(H)], axis=1)
    xi = np.broadcast_to(np.stack([gamma[h] ** (pos + 1.0) for h in range(H)], axis=0)[None], (128, H, 128))
    ci = np.arange(128) // 64
    wmask = (ci[:, None] <= ci[None, :]).astype(np.float32)
    inv = (10000.0 ** (-np.arange(0, 256, 2, dtype=np.float32) / 256.0)).astype(np.float32)
    ang = (np.arange(SEQ, dtype=np.float32)[:, None] * inv[None, :]).astype(np.float32)
    rope = np.stack([np.cos(ang).T, np.sin(ang).T]).astype(np.float32)
    return maskT.astype(np.float32), zeta.astype(np.float32), np.ascontiguousarray(xi, dtype=np.float32), wmask, rope


def prep_inputs(inp):
    f = lambda k: np.asarray(inp[k], np.float32)
    maskT, zeta, xi, wmask, rope = _consts()
    cst = np.zeros((128, NCST), np.float32)

    def put(name, arr):
        a = np.ascontiguousarray(arr, dtype=np.float32).reshape(128, -1)
        o = CST_OFF[name]
        cst[:, o:o + a.shape[1]] = a

    put("mixg", _pm(f("mix_norm_g"), 16))
    put("ffng", _pm(f("ffn_norm_g"), 16))
    put("fing", _pm(f("final_norm_g"), 16))
    put("fdw", _pm(f("ffn_dw_w"), 88))
    put("fdb", _pm(f("ffn_dw_b"), 88))
    put("cdw", _pm(f("ev_conv_dw_w"), 8))
    put("cdb", _pm(f("ev_conv_dw_b"), 8))
    put("clg", _pm(f("ev_conv_ln_g"), 8))
    put("clb", _pm(f("ev_conv_ln_b"), 8))
    put("rlg", _pm(f("ev_ret_ln_g"), 8))
    put("glg", _pm(f("od_gm_ln_g"), 16))
    put("glb", _pm(f("od_gm_ln_b"), 16))
    put("zeta", zeta)
    put("maskT", maskT)
    put("xi", xi)
    put("bsbc", np.broadcast_to(f("od_gm_bs")[None], (128, 2, 4, 128)))
    put("wmask", wmask)
    put("ident", np.eye(128, dtype=np.float32))
    wsT = np.ascontiguousarray(np.transpose(f("od_gm_ws"), (3, 0, 1, 2))).reshape(128, 1024)

    wstream = np.zeros((NPIECE, 128, 8192), np.float32)
    ar = np.arange
    for li in range(4):
        j = li // 2
        base = PIECE_BASE[li]
        for pi, kd in enumerate(layer_pieces(li)):
            dst = wstream[base + pi]
            k0, k1 = kd
            if k0 == "glu":
                W = f("ev_w_in")[j]
                dst[:] = _col_piece(W, np.concatenate([ar(k1 * 256, (k1 + 1) * 256), 1024 + ar(k1 * 256, (k1 + 1) * 256)]))
            elif k0 in ("q", "k", "v", "gate"):
                W = f("ev_w_in")[j]
                off = {"q": 2048, "k": 3072, "v": 4096, "gate": 5120}[k0]
                dst[:] = _col_piece(W, off + ar(k1 * 512, (k1 + 1) * 512))
            elif k0 == "mout":
                W = f("ev_w_out")[j] if li % 2 == 0 else f("od_w_out")[j]
                dst[:] = _col_piece(W, ar(k1 * 512, (k1 + 1) * 512))
            elif k0 == "gu":
                dst[:] = _col_piece(f("od_w_in")[j], ar(k1 * 512, (k1 + 1) * 512))
            elif k0 == "gv":
                dst[:] = _col_piece(f("od_w_in")[j], 2048 + ar(k1 * 512, (k1 + 1) * 512))
            elif k0 == "up":
                W = f("ffn_w_up")[li]
                dst[:] = _col_piece(W, np.concatenate([ar(k1 * 256, (k1 + 1) * 256), DFF + ar(k1 * 256, (k1 + 1) * 256)]))
            elif k0 == "down":
                Wd = f("ffn_w_down")[li]
                blk = Wd[:, k1 * 128:(k1 + 1) * 128]
                dst[:, :5632] = blk.reshape(44, 128, 128).transpose(1, 0, 2).reshape(128, 5632)
    return {"wstream": wstream, "cst": cst, "wsT": wsT, "rope": rope}


_PROG_CACHE = {}


def kernel(**inputs):
    x = np.asarray(inputs["x"], np.float32)
    shared = prep_inputs(inputs)
    if "nc" not in _PROG_CACHE:
        _PROG_CACHE["nc"] = build_program()
    nc = _PROG_CACHE["nc"]
    in_maps = []
    for c in range(NCORE):
        xc = x[c * NSEQ:(c + 1) * NSEQ].reshape(NSEQ * SEQ, D)
        m = dict(shared)
        m["xT"] = np.ascontiguousarray(xc.T)
        in_maps.append(m)
    res = run_bass_kernel_spmd(nc, in_maps, core_ids=list(range(NCORE)))
    out = np.empty((NCORE * NSEQ, SEQ, D), np.float32)
    for c in range(NCORE):
        out[c * NSEQ:(c + 1) * NSEQ] = np.asarray(res.results[c]["outT"]).T.reshape(NSEQ, SEQ, D)
    return out
```

```python
import numpy as np
from contextlib import ExitStack
import concourse.bass as bass
import concourse.mybir as mybir
from concourse.bass_utils import run_bass_kernel_spmd

F32 = mybir.dt.float32
BF16 = mybir.dt.bfloat16
AF = mybir.ActivationFunctionType
ALU = mybir.AluOpType

D = 2048
KC = 16
T = 512
NB = T // 128
SEQ = 2048
TPS = SEQ // T
NSEQ = 2
NCORE = 8
DFF = 5632
NSLOT = 2
RMS_EPS = 1e-6
LN_EPS = 1e-5
ARENA_W = 69 * 256


class Buf:
    __slots__ = ("last_write", "readers")

    def __init__(self, readers=None):
        self.last_write = None
        self.readers = list(readers) if readers else []


class Prog:
    ENGS = ("pe", "act", "dve", "pool", "sp")

    def __init__(self, nc, stack):
        self.nc = nc
        self.stack = stack
        self.ops = {e: [] for e in self.ENGS}
        self.sem = {}
        self.sems = []
        self.semval = []
        for e in self.ENGS:
            self.sem[e] = self.new_sem("s_" + e)
        self.seen = {e: {} for e in self.ENGS}
        self.n_ops = 0
        self.n_wait = 0

    def new_sem(self, name):
        h = self.stack.enter_context(self.nc.semaphore(name))
        self.sems.append(h)
        self.semval.append(0)
        return len(self.sems) - 1

    def barrier(self):
        return [(self.sem[e], self.semval[self.sem[e]]) for e in ("pe", "act", "dve") if self.semval[self.sem[e]] > 0]

    def op(self, eng, fn, reads=(), writes=(), deps=(), sem=None, inc=1, mark=True):
        need = {}

        def add(tok):
            if tok is None:
                return
            s, v = tok
            if need.get(s, 0) < v:
                need[s] = v

        for b in reads:
            add(b.last_write)
        for b in writes:
            add(b.last_write)
            for t in b.readers:
                add(t)
        for t in deps:
            add(t)
        waits = []
        seen = self.seen[eng]
        own = self.sem[eng]
        for s, v in need.items():
            if s == own and eng == "pe":
                continue
            if seen.get(s, 0) >= v:
                continue
            seen[s] = v
            waits.append((s, v))
        if mark:
            si = own if sem is None else sem
            self.semval[si] += inc
            tok = (si, self.semval[si])
            inc_si = si
        else:
            tok = (own, self.semval[own] + 1)
            inc_si = None
        sems = self.sems
        self.n_ops += 1
        self.n_wait += len(waits)

        def emit(e, waits=waits, fn=fn, inc_si=inc_si, inc=inc):
            for s, v in waits:
                e.wait_ge(sems[s], v)
            ins = fn(e)
            if inc_si is not None:
                ins.then_inc(sems[inc_si], inc)

        self.ops[eng].append(emit)
        for b in reads:
            b.readers.append(tok)
            if len(b.readers) > 48:
                mx = {}
                for t in b.readers:
                    if mx.get(t[0], 0) < t[1]:
                        mx[t[0]] = t[1]
                b.readers = list(mx.items())
        for b in writes:
            b.last_write = tok
            b.readers = []
        return tok

    def wait_all(self, eng, toks):
        sems = self.sems
        toks = [t for t in toks if t is not None]

        def emit(e):
            for s, v in toks:
                e.wait_ge(sems[s], v)

        self.ops[eng].append(emit)

    def finish(self):
        with self.nc.Block() as block:
            @block.tensor
            def _(e):
                for f in self.ops["pe"]:
                    f(e)

            @block.scalar
            def _(e):
                for f in self.ops["act"]:
                    f(e)

            @block.vector
            def _(e):
                for f in self.ops["dve"]:
                    f(e)

            @block.gpsimd
            def _(e):
                for f in self.ops["pool"]:
                    f(e)

            @block.sync
            def _(e):
                for f in self.ops["sp"]:
                    f(e)


CST_SPEC = [
    ("mixg", 4 * 16), ("ffng", 4 * 16), ("fing", 16),
    ("fdw", 4 * 3 * 88), ("fdb", 4 * 88),
    ("cdw", 2 * 31 * 8), ("cdb", 2 * 8), ("clg", 2 * 8), ("clb", 2 * 8), ("rlg", 2 * 8),
    ("glg", 2 * 16), ("glb", 2 * 16),
    ("zeta", 4), ("maskT", 4 * 128), ("xi", 4 * 128), ("bsbc", 2 * 4 * 128),
    ("wmask", 128), ("ident", 128),
]
CST_OFF = {}
_o = 0
for _n, _w in CST_SPEC:
    CST_OFF[_n] = _o
    _o += _w
NCST = _o


def layer_pieces(li):
    out = []
    if li % 2 == 0:
        out += [("glu", gp) for gp in range(4)]
        for hp in range(2):
            out += [("q", hp), ("k", hp), ("v", hp), ("gate", hp)]
        out += [("mout", m) for m in range(4)]
    else:
        out += [("gu", m) for m in range(4)]
        out += [("gv", m) for m in range(4)]
        out += [("mout", m) for m in range(4)]
    out += [("up", i) for i in range(22)]
    out += [("down", m) for m in range(16)]
    return out


PIECE_BASE = []
_b = 0
for _li in range(4):
    PIECE_BASE.append(_b)
    _b += len(layer_pieces(_li))
NPIECE = _b


def build_program(n_tiles=NSEQ * TPS, layers=(0, 1, 2, 3), do_ffn=True, final_norm=True):
    nc = bass.Bass("TRN2", target_bir_lowering=False)
    xT = nc.dram_tensor("xT", [D, NSEQ * SEQ], F32, kind="ExternalInput").ap()
    wst = nc.dram_tensor("wstream", [NPIECE, 128, 8192], F32, kind="ExternalInput").ap()
    cst_d = nc.dram_tensor("cst", [128, NCST], F32, kind="ExternalInput").ap()
    wsT_d = nc.dram_tensor("wsT", [128, 1024], F32, kind="ExternalInput").ap()
    rope_d = nc.dram_tensor("rope", [2, 128, SEQ], F32, kind="ExternalInput").ap()
    outT = nc.dram_tensor("outT", [D, NSEQ * SEQ], F32, kind="ExternalOutput").ap()
    xT_v = xT.rearrange("(k p) t -> p k t", p=128)
    outT_v = outT.rearrange("(k p) t -> p k t", p=128)

    with ExitStack() as st:
        P = Prog(nc, st)
        sb = lambda name, shape, dt: st.enter_context(nc.sbuf_tensor(name, shape, dt))
        x_sb = sb("x_sb", [128, KC, T], F32)
        h_sb = sb("h_sb", [128, KC, T], BF16)
        slots = [sb(f"wslot{i}", [128, 8192], BF16) for i in range(NSLOT)]
        cst = sb("cst_sb", [128, NCST], F32)
        cos_sb = sb("cos_sb", [128, T], F32)
        sin_sb = sb("sin_sb", [128, T], F32)
        S_f32 = sb("S_f32", [128, 2, 4, 512], F32)
        S_bf = sb("S_bf", [128, 4, 512], BF16)
        uh = sb("uh", [128, 2, 8, 30], F32)
        fh = sb("fh", [128, 4, 88, 2], F32)
        ident = sb("ident", [128, 128], BF16)
        ones = sb("ones", [128, 128], BF16)
        wmT = sb("wmT", [128, 2, 4, 128], BF16)
        rowsum = sb("rowsum", [128, 2, 4, 128], F32)
        arena = sb("arena", [128, ARENA_W], F32)
        ps = [st.enter_context(nc.psum_tensor(f"ps{i}", [128, 512], F32)) for i in range(8)]

        xb = [Buf() for _ in range(KC)]
        hb = [Buf() for _ in range(KC)]
        pb = [Buf() for _ in range(8)]
        slot_b = [[Buf(), Buf()] for _ in range(NSLOT)]
        slot_sem = [[P.new_sem(f"slot{i}_{h}") for h in range(2)] for i in range(NSLOT)]
        cst_b = Buf()
        cst_sem = P.new_sem("cst")
        cos_b, sin_b = Buf(), Buf()
        cos_sem, sin_sem = P.new_sem("cos"), P.new_sem("sin")
        x_sems = [P.new_sem(f"xld{i}") for i in range(4)]
        o_sem = P.new_sem("ost")
        Sf_b = [[Buf() for _ in range(4)] for _ in range(2)]
        Sb_b = [Buf() for _ in range(4)]
        uh_b = [[Buf() for _ in range(8)] for _ in range(2)]
        fh_b = [Buf() for _ in range(4)]
        misc_b = Buf()

        def C(name, idx=0, n=1):
            o = CST_OFF[name] + idx
            return cst[:, o:o + n]

        extra_bar = []
        class Arena:
            def __init__(self, start=0):
                self.off = start
                self.bar = P.barrier() + list(extra_bar)

            def f32(self, n):
                a = arena[:, self.off:self.off + n]
                self.off += n
                assert self.off <= ARENA_W, self.off
                return a, Buf(self.bar)

            def bf16(self, n):
                w = (n + 1) // 2
                a = arena[:, self.off:self.off + w].bitcast(BF16)
                self.off += w
                assert self.off <= ARENA_W, self.off
                return a, Buf(self.bar)

        pass_pieces = []
        for li in layers:
            kinds = layer_pieces(li)
            for i, kd in enumerate(kinds):
                if not do_ffn and kd[0] in ("up", "down"):
                    continue
                pass_pieces.append((PIECE_BASE[li] + i, 5632 if kd[0] == "down" else 8192, (li,) + kd))
        total_pieces = n_tiles * len(pass_pieces)
        wstate = {"load": 0, "use": 0}
        slot_cls = [None] * NSLOT

        def w_load():
            p = wstate["load"]
            if p >= total_pieces:
                return
            hidx, n, kd = pass_pieces[p % len(pass_pieces)]
            s = p % NSLOT
            kind = kd[1]
            cls = "down" if kind == "down" else ("mov" if kind in ("v", "gate", "gv") else "col")
            xdeps = []
            if slot_cls[s] is not None and slot_cls[s] != cls:
                xdeps = list(slot_b[s][1].readers) + [slot_b[s][1].last_write]
            slot_cls[s] = cls
            for hf in range(2):
                if kind == "down":
                    o_ap = slots[s][:, hf * 2816:(hf + 1) * 2816]
                    i_ap = wst[hidx, :, hf * 2816:(hf + 1) * 2816]
                elif kind in ("v", "gate", "gv"):
                    o_ap = slots[s][:, hf * 4096:(hf + 1) * 4096]
                    i_ap = wst[hidx, :, hf * 4096:(hf + 1) * 4096]
                else:
                    o_ap = slots[s][:].rearrange("p (k c) -> p k c", k=KC)[:, :, hf * 256:(hf + 1) * 256]
                    i_ap = wst[hidx].rearrange("p (k c) -> p k c", k=KC)[:, :, hf * 256:(hf + 1) * 256]
                P.op("pool", lambda e, o_ap=o_ap, i_ap=i_ap: e.dma_start(out=o_ap, in_=i_ap),
                     writes=[slot_b[s][hf]], deps=(xdeps if hf == 0 else ()), sem=slot_sem[s][hf], inc=16)
            wstate["load"] += 1

        def w_acquire(expect):
            p = wstate["use"]
            _, _, kd = pass_pieces[p % len(pass_pieces)]
            assert kd == expect, (kd, expect)
            s = p % NSLOT
            return slots[s], slot_b[s]

        def w_release():
            wstate["use"] += 1
            w_load()

        bank_rr = {"i": 0}

        def nbank():
            b = bank_rr["i"]
            bank_rr["i"] = (b + 1) % 6
            return b

        def mm(out, lhsT, rhs, start, stop, reads, bank, mark=None):
            if mark is None:
                mark = stop
            return P.op("pe", lambda e: e.matmul(out, lhsT, rhs, start=start, stop=stop),
                        reads=reads, writes=[pb[bank]], mark=mark)

        def act(out, in_, func, reads, writes, bias=None, scale=None):
            kw = {}
            if bias is not None:
                kw["bias"] = bias
            if scale is not None:
                kw["scale"] = scale
            return P.op("act", lambda e: e.activation(out, in_, func, **kw), reads=reads, writes=writes)

        def tt(out, in0, in1, op, reads, writes):
            return P.op("dve", lambda e: e.tensor_tensor(out, in0, in1, op), reads=reads, writes=writes)

        def stt(out, in0, scalar, in1, op0, op1, reads, writes):
            return P.op("dve", lambda e: e.scalar_tensor_tensor(out, in0, scalar, in1, op0, op1), reads=reads, writes=writes)

        def ts(out, in0, s1, s2, op0, op1, reads, writes):
            return P.op("dve", lambda e: e.tensor_scalar(out, in0, s1, s2, op0, op1), reads=reads, writes=writes)

        def recip(out, in_, reads, writes):
            return P.op("dve", lambda e: e.reciprocal(out, in_), reads=reads, writes=writes)

        P.op("sp", lambda e: e.dma_start(out=cst[:], in_=cst_d), writes=[cst_b], sem=cst_sem, inc=16)
        for _ in range(NSLOT):
            w_load()
        ar = Arena()
        wtmp, wtmp_b = ar.f32(1024)
        wt_sem = P.new_sem("wtmp")
        P.op("sp", lambda e: e.dma_start(out=wtmp, in_=wsT_d), writes=[wtmp_b], sem=wt_sem, inc=16)
        P.op("dve", lambda e: e.memset(ones[:], 1.0), writes=[misc_b])
        P.op("act", lambda e: e.copy(out=ident[:], in_=C("ident", 0, 128)), reads=[cst_b], writes=[misc_b])
        tt(wmT[:].rearrange("p j g i -> p (j g) i"), wtmp.rearrange("p (a i) -> p a i", a=8),
           C("wmask", 0, 128).unsqueeze(1).to_broadcast([128, 8, 128]), ALU.mult, [wtmp_b, cst_b], [misc_b])
        for j in range(2):
            mm(ps[j][:], ones[:], wmT[:, j].rearrange("p g i -> p (g i)"), True, True, [misc_b], j)
            P.op("act", lambda e, j=j: e.copy(out=rowsum[:, j].rearrange("p g i -> p (g i)"), in_=ps[j][:]),
                 reads=[], writes=[pb[j], misc_b])

        def rmsnorm(gname, gidx, in_place=False):
            ar = Arena()
            sq = [ar.bf16(T) for _ in range(2)]
            sd, sd_b = ar.f32(T)
            rstd, rstd_b = ar.f32(T)
            for kc in range(KC):
                a, ab = sq[kc % 2]
                act(a, x_sb[:, kc, :], AF.Square, [xb[kc]], [ab])
                mm(ps[6][:], ones[:], a, kc == 0, kc == KC - 1, [ab, misc_b], 6, mark=True)
            act(sd, ps[6][:], AF.Sqrt, [], [pb[6], sd_b], bias=RMS_EPS, scale=1.0 / D)
            recip(rstd, sd, [sd_b], [rstd_b])
            if in_place:
                stg, stg_b = ar.f32(KC * T)
            for kc in range(KC):
                if in_place:
                    stt(stg[:, kc * T:(kc + 1) * T], x_sb[:, kc, :], C(gname, gidx * 16 + kc), rstd, ALU.mult, ALU.mult,
                        [rstd_b, cst_b, xb[kc]], [stg_b])
                else:
                    stt(h_sb[:, kc, :], x_sb[:, kc, :], C(gname, gidx * 16 + kc), rstd, ALU.mult, ALU.mult,
                        [rstd_b, cst_b, xb[kc]], [hb[kc]])
            if in_place:
                return stg, stg_b

        def proj_fm(sv, sbuf, m, bank, rhs_of, rbufs_of, nk=KC):
            for kc in range(nk):
                mm(ps[bank][:], sv[:, kc, m * 128:(m + 1) * 128], rhs_of(kc), kc == 0, kc == nk - 1,
                   [sbuf[m // 2]] + rbufs_of(kc), bank)

        def out_proj(li, cat, catb):
            for mp in range(4):
                slot, sbuf = w_acquire((li, "mout", mp))
                sv = slot[:].rearrange("p (k c) -> p k c", k=KC)
                for m in range(4):
                    mo = mp * 4 + m
                    bank = nbank()
                    proj_fm(sv, sbuf, m, bank, lambda kc: cat[:, kc, :], lambda kc: [catb[kc]])
                    tt(x_sb[:, mo, :], x_sb[:, mo, :], ps[bank][:], ALU.add, [], [xb[mo], pb[bank]])
                w_release()

        def even_mixer(li):
            j = li // 2
            ar = Arena()
            cat_f, _ = ar.bf16(KC * T)
            cat = cat_f.rearrange("p (k t) -> p k t", k=KC)
            catb = [Buf(ar.bar) for _ in range(KC)]
            cat_end = ar.off
            c_f, _ = ar.f32(8 * T)
            c_sb = c_f.rearrange("p (k t) -> p k t", k=8)
            cb = [Buf(ar.bar) for _ in range(8)]
            ub_f, _ = ar.bf16(8 * (32 + T))
            u_bf = ub_f.rearrange("p (k t) -> p k t", k=8)
            uib = [Buf(ar.bar) for _ in range(8)]
            tb = [ar.bf16(T) for _ in range(4)]
            ln_off = ar.off
            sig = [ar.f32(T) for _ in range(2)]
            dg = []
            for _ in range(2):
                d_f, d_b = ar.bf16(31 * 128)
                dg.append((d_f.rearrange("p (k i) -> p k i", k=31), d_b))
            def glu_chunk(sv, sbuf, gp, cc):
                c = 2 * gp + cc
                bA, bB = nbank(), nbank()
                proj_fm(sv, sbuf, 2 * cc, bA, lambda kc: h_sb[:, kc, :], lambda kc: [hb[kc]])
                proj_fm(sv, sbuf, 2 * cc + 1, bB, lambda kc: h_sb[:, kc, :], lambda kc: [hb[kc]])
                sg, sgb = sig[c % 2]
                act(sg, ps[bB][:], AF.Sigmoid, [], [pb[bB], sgb])
                P.op("act", lambda e, c=c: e.copy(out=u_bf[:, c, 0:30], in_=uh[:, j, c, :]),
                     reads=[uh_b[j][c]], writes=[uib[c]])
                tt(u_bf[:, c, 30:30 + T], ps[bA][:], sg, ALU.mult, [sgb], [pb[bA], uib[c]])
                P.op("act", lambda e, c=c: e.copy(out=uh[:, j, c, :], in_=u_bf[:, c, T:T + 30]),
                     reads=[uib[c]], writes=[uh_b[j][c]])
                dgc, dgb = dg[c % 2]
                o = CST_OFF["cdw"] + j * 31 * 8 + c
                wk = cst[:, o:o + 31 * 8:8]
                tt(dgc, ident[:].unsqueeze(1).to_broadcast([128, 31, 128]),
                   wk.unsqueeze(2).to_broadcast([128, 31, 128]), ALU.mult, [misc_b, cst_b], [dgb])

            def conv_chunk(c):
                dgc, dgb = dg[c % 2]
                bank = nbank()
                for k in range(31):
                    mm(ps[bank][:], dgc[:, k, :], u_bf[:, c, k:k + T], k == 0, k == 30, [dgb, uib[c]], bank)
                a, ab = tb[(2 * c) % 4]
                q, qb = tb[(2 * c + 1) % 4]
                bia = C("cdb", j * 8 + c)
                act(c_sb[:, c, :], ps[bank][:], AF.Identity, [cst_b], [pb[bank], cb[c]], bias=bia, scale=1.0)
                act(a, ps[bank][:], AF.Identity, [cst_b], [pb[bank], ab], bias=bia, scale=1.0)
                act(q, ps[bank][:], AF.Square, [cst_b], [pb[bank], qb], bias=bia, scale=1.0)
                def stat_mm(a=a, ab=ab, q=q, qb=qb, c=c):
                    mm(ps[6][:], ones[:], a, c == 0, c == 7, [ab, misc_b], 6, mark=True)
                    mm(ps[7][:], ones[:], q, c == 0, c == 7, [qb, misc_b], 7, mark=True)
                if pstat[0] is not None:
                    pstat[0]()
                pstat[0] = stat_mm

            pstat = [None]
            pend = None
            for gp in range(4):
                slot, sbuf = w_acquire((li, "glu", gp))
                sv = slot[:].rearrange("p (k c) -> p k c", k=KC)
                for cc in range(2):
                    glu_chunk(sv, sbuf, gp, cc)
                    if pend is not None:
                        conv_chunk(pend)
                    pend = 2 * gp + cc
                w_release()
            conv_chunk(pend)
            pstat[0]()
            ar = Arena(ln_off)
            mean, mean_b = ar.f32(T)
            msq, msq_b = ar.f32(T)
            rstd, rstd_b = ar.f32(T)
            act(mean, ps[6][:], AF.Identity, [], [pb[6], mean_b], scale=1.0 / 1024)
            tt(msq, mean, mean, ALU.mult, [mean_b], [msq_b])
            stt(msq, ps[7][:], 1.0 / 1024, msq, ALU.mult, ALU.subtract, [], [pb[7], msq_b])
            act(msq, msq, AF.Sqrt, [], [msq_b], bias=LN_EPS, scale=1.0)
            recip(rstd, msq, [msq_b], [rstd_b])
            for c in range(8):
                tt(c_sb[:, c, :], c_sb[:, c, :], mean, ALU.subtract, [mean_b], [cb[c]])
                tt(c_sb[:, c, :], c_sb[:, c, :], rstd, ALU.mult, [rstd_b], [cb[c]])
                act(cat[:, c, :], c_sb[:, c, :], AF.Silu, [cb[c], cst_b], [catb[c]],
                    bias=C("clb", j * 8 + c), scale=C("clg", j * 8 + c))
            ar = Arena(cat_end)
            qT_f, qT_b = ar.bf16(4 * T)
            kT_f, kT_b = ar.bf16(4 * T)
            qx_f, qx_b = ar.bf16(4 * T)
            qT = qT_f.rearrange("p (k t) -> p k t", k=4)
            kT = kT_f.rearrange("p (k t) -> p k t", k=4)
            qx = qx_f.rearrange("p (k t) -> p k t", k=4)
            v_f, v_b = ar.bf16(NB * 512)
            sg_f, sg_b = ar.bf16(NB * 512)
            kz_f, kz_b = ar.bf16(NB * 512)
            v_sb = v_f.rearrange("p (b e) -> p b e", b=NB)
            sg_sb = sg_f.rearrange("p (b e) -> p b e", b=NB)
            kz = kz_f.rearrange("p (b e) -> p b e", b=NB)
            t1, t1_b = ar.f32(T)
            t2, t2_b = ar.f32(T)
            sc = [ar.bf16(128) for _ in range(8)]
            on = [ar.f32(256) for _ in range(4)]
            bo = [ar.bf16(256) for _ in range(4)]
            stats = [ar.f32(8) for _ in range(4)]
            mv = [ar.f32(4) for _ in range(4)]
            for hp in range(2):
                for nm, dst, dst_b in (("q", qT, qT_b), ("k", kT, kT_b)):
                    slot, sbuf = w_acquire((li, nm, hp))
                    sv = slot[:].rearrange("p (k c) -> p k c", k=KC)
                    for hl in range(2):
                        bA, bB = nbank(), nbank()
                        proj_fm(sv, sbuf, 2 * hl, bA, lambda kc: h_sb[:, kc, :], lambda kc: [hb[kc]])
                        proj_fm(sv, sbuf, 2 * hl + 1, bB, lambda kc: h_sb[:, kc, :], lambda kc: [hb[kc]])
                        tt(t1, ps[bA][:], cos_sb[:], ALU.mult, [cos_b], [pb[bA], t1_b])
                        tt(t2, ps[bB][:], sin_sb[:], ALU.mult, [sin_b], [pb[bB], t2_b])
                        tt(dst[:, 2 * hl, :], t1, t2, ALU.subtract, [t1_b, t2_b], [dst_b])
                        tt(t1, ps[bA][:], sin_sb[:], ALU.mult, [sin_b], [pb[bA], t1_b])
                        tt(t2, ps[bB][:], cos_sb[:], ALU.mult, [cos_b], [pb[bB], t2_b])
                        tt(dst[:, 2 * hl + 1, :], t1, t2, ALU.add, [t1_b, t2_b], [dst_b])
                    w_release()
                    if nm == "q":
                        for hl in range(2):
                            hh = 2 * hp + hl
                            for dc in range(2):
                                tt(qx[:, 2 * hl + dc, :].rearrange("p (b n) -> p b n", b=NB),
                                   qT[:, 2 * hl + dc, :].rearrange("p (b n) -> p b n", b=NB),
                                   C("xi", hh * 128, 128).unsqueeze(1).to_broadcast([128, NB, 128]),
                                   ALU.mult, [qT_b, cst_b], [qx_b])
                    else:
                        for hl in range(2):
                            hh = 2 * hp + hl
                            for b in range(NB):
                                bank = nbank()
                                pv = ps[bank][:, 0:128].bitcast(BF16)
                                for dc in range(2):
                                    P.op("pe", lambda e, pv=pv, dc=dc, hl=hl, b=b: e.transpose(
                                        pv[:, dc * 128:(dc + 1) * 128], kT[:, 2 * hl + dc, b * 128:(b + 1) * 128], ident[:]),
                                        reads=[kT_b, misc_b], writes=[pb[bank]])
                                act(kz[:, b, hl * 256:(hl + 1) * 256], pv, AF.Identity, [cst_b], [pb[bank], kz_b],
                                    scale=C("zeta", hh))
                for nm, dst, dst_b, fn in (("v", v_sb, v_b, AF.Identity), ("gate", sg_sb, sg_b, AF.Silu)):
                    slot, sbuf = w_acquire((li, nm, hp))
                    sv = slot[:].rearrange("p (k c) -> p k c", k=KC)
                    for b in range(NB):
                        bank = nbank()
                        for kc in range(KC):
                            mm(ps[bank][:], h_sb[:, kc, b * 128:(b + 1) * 128], sv[:, kc, :], kc == 0, kc == KC - 1,
                               [sbuf[kc // 8], hb[kc]], bank)
                        act(dst[:, b, :], ps[bank][:], fn, [], [pb[bank], dst_b])
                    w_release()
                for step in range(2 * NB):
                    b, hl = step // 2, step % 2
                    blk = slice(b * 128, (b + 1) * 128)
                    bsc = 0 if step < 4 else 4
                    cs = slice((step % 4) * 128, (step % 4 + 1) * 128)
                    for dc in range(2):
                        mm(ps[bsc][:, cs], kT[:, 2 * hl + dc, blk], qT[:, 2 * hl + dc, blk], dc == 0, dc == 1,
                           [kT_b, qT_b], bsc)
                for step in range(2 * NB):
                    hl = step % 2
                    hh = 2 * hp + hl
                    bsc = 0 if step < 4 else 4
                    cs = slice((step % 4) * 128, (step % 4 + 1) * 128)
                    scs, scb = sc[step]
                    tt(scs, ps[bsc][:, cs], C("maskT", hh * 128, 128), ALU.mult, [cst_b], [pb[bsc], scb])
                pend_T = None
                for b in range(NB):
                    blk = slice(b * 128, (b + 1) * 128)
                    for hl in range(2):
                        hh = 2 * hp + hl
                        step = 2 * b + hl
                        bo_, bst, bT = 4 * hl + 1, 4 * hl + 2, 4 * hl + 3
                        scs, scb = sc[step]
                        for dc in range(2):
                            mm(ps[bo_][:, 0:256], qx[:, 2 * hl + dc, blk], S_bf[:, hh, dc * 256:(dc + 1) * 256],
                               dc == 0, False, [qx_b, Sb_b[hh]], bo_, mark=False)
                        mm(ps[bo_][:, 0:256], scs, v_sb[:, b, hl * 256:(hl + 1) * 256], False, True, [scb, v_b], bo_)
                        for dc in range(2):
                            mm(ps[bst][:, dc * 256:(dc + 1) * 256], kz[:, b, hl * 256 + dc * 128: hl * 256 + (dc + 1) * 128],
                               v_sb[:, b, hl * 256:(hl + 1) * 256], True, True, [kz_b, v_b], bst, mark=(dc == 1))
                        if pend_T is not None:
                            pend_T()
                        gk = float((1.0 - 2.0 ** (-5 - hh)) ** 128)
                        stt(S_f32[:, j, hh, :], S_f32[:, j, hh, :], gk, ps[bst][:], ALU.mult, ALU.add,
                            [], [Sf_b[j][hh], pb[bst]])
                        P.op("act", lambda e, hh=hh: e.copy(out=S_bf[:, hh, :], in_=S_f32[:, j, hh, :]),
                             reads=[Sf_b[j][hh]], writes=[Sb_b[hh]])
                        sta, stb = stats[step % 4]
                        mva, mvb = mv[step % 4]
                        ona, onb = on[step % 4]
                        boa, bob = bo[step % 4]
                        P.op("dve", lambda e, sta=sta, bo_=bo_: e.bn_stats(sta[:, 0:6], ps[bo_][:, 0:256]),
                             reads=[], writes=[pb[bo_], stb])
                        P.op("dve", lambda e, sta=sta, mva=mva: e.bn_aggr(mva[:, 0:2], sta[:, 0:6]), reads=[stb], writes=[mvb])
                        act(mva[:, 2:3], mva[:, 1:2], AF.Sqrt, [], [mvb], bias=LN_EPS, scale=1.0)
                        recip(mva[:, 3:4], mva[:, 2:3], [], [mvb])
                        ts(ona, ps[bo_][:, 0:256], mva[:, 0:1], mva[:, 3:4], ALU.subtract, ALU.mult, [mvb], [pb[bo_], onb])
                        tt(boa, ona, sg_sb[:, b, hl * 256:(hl + 1) * 256], ALU.mult, [onb, sg_b], [bob])

                        def do_T(boa=boa, bob=bob, bT=bT, hh=hh, blk=blk):
                            pv = ps[bT][:, 0:128].bitcast(BF16)
                            for ec in range(2):
                                P.op("pe", lambda e, pv=pv, ec=ec, boa=boa: e.transpose(
                                    pv[:, ec * 128:(ec + 1) * 128], boa[:, ec * 128:(ec + 1) * 128], ident[:]),
                                    reads=[bob, misc_b], writes=[pb[bT]])
                            for ec in range(2):
                                cidx = 8 + 2 * hh + ec
                                act(cat[:, cidx, blk], pv[:, ec * 128:(ec + 1) * 128], AF.Identity, [cst_b],
                                    [pb[bT], catb[cidx]], scale=C("rlg", j * 8 + 2 * hh + ec))
                        pend_T = do_T
                pend_T()
            out_proj(li, cat, catb)

        def odd_mixer(li):
            j = li // 2
            ar = Arena()
            uT_f, _ = ar.bf16(KC * T)
            uT = uT_f.rearrange("p (k t) -> p k t", k=KC)
            ub = [Buf(ar.bar) for _ in range(KC)]
            vt_f, _ = ar.f32(NB * 2048)
            v_tok = vt_f.rearrange("p (b f) -> p b f", b=NB)
            vtb = [Buf(ar.bar) for _ in range(NB)]
            bias_f, bias_b = ar.f32(16 * 128)
            bias_t = bias_f.rearrange("p (f i) -> p f i", f=16)
            tmp = [ar.f32(512) for _ in range(4)]
            stats, stats_b = ar.f32(24)
            mv, mv_b = ar.f32(4)
            for fc in range(16):
                g = fc // 4
                stt(bias_t[:, fc, :], rowsum[:, j, g, :], C("glb", j * 16 + fc), C("bsbc", (j * 4 + g) * 128, 128),
                    ALU.mult, ALU.add, [misc_b, cst_b], [bias_b])
            for mp in range(4):
                slot, sbuf = w_acquire((li, "gu", mp))
                sv = slot[:].rearrange("p (k c) -> p k c", k=KC)
                for m in range(4):
                    mo = mp * 4 + m
                    bank = nbank()
                    proj_fm(sv, sbuf, m, bank, lambda kc: h_sb[:, kc, :], lambda kc: [hb[kc]])
                    act(uT[:, mo, :], ps[bank][:], AF.Gelu, [], [pb[bank], ub[mo]])
                w_release()
            for cg in range(4):
                slot, sbuf = w_acquire((li, "gv", cg))
                sv = slot[:].rearrange("p (k c) -> p k c", k=KC)
                for b in range(NB):
                    bank = nbank()
                    for kc in range(KC):
                        mm(ps[bank][:], h_sb[:, kc, b * 128:(b + 1) * 128], sv[:, kc, :], kc == 0, kc == KC - 1,
                           [sbuf[kc // 8], hb[kc]], bank)
                    act(v_tok[:, b, cg * 512:(cg + 1) * 512], ps[bank][:], AF.Gelu, [], [pb[bank], vtb[b]])
                w_release()
            vn = h_sb[:].rearrange("p k t -> p (k t)").rearrange("p (b f) -> p b f", b=NB)
            for b in range(NB):
                for cg in range(4):
                    P.op("dve", lambda e, b=b, cg=cg: e.bn_stats(stats[:, cg * 6:(cg + 1) * 6], v_tok[:, b, cg * 512:(cg + 1) * 512]),
                         reads=[vtb[b]], writes=[stats_b])
                P.op("dve", lambda e: e.bn_aggr(mv[:, 0:2], stats[:, 0:24]), reads=[stats_b], writes=[mv_b])
                act(mv[:, 2:3], mv[:, 1:2], AF.Sqrt, [], [mv_b], bias=LN_EPS, scale=1.0)
                recip(mv[:, 3:4], mv[:, 2:3], [], [mv_b])
                ts(vn[:, b, :], v_tok[:, b, :], mv[:, 0:1], mv[:, 3:4], ALU.subtract, ALU.mult, [mv_b, vtb[b]],
                   [hb[4 * b + i] for i in range(4)])
            for b in range(NB):
                blk = slice(b * 128, (b + 1) * 128)
                for g in range(4):
                    bank = nbank()
                    for fl in range(4):
                        fc = 4 * g + fl
                        mm(ps[bank][:, fl * 128:(fl + 1) * 128], vn[:, b, fc * 128:(fc + 1) * 128], wmT[:, j, g, :], True, True,
                           [hb[4 * b + fc // 4], misc_b], bank, mark=(fl == 3))
                    tm, tmb = tmp[(b * 4 + g) % 4]
                    for fl in range(4):
                        fc = 4 * g + fl
                        act(tm[:, fl * 128:(fl + 1) * 128], ps[bank][:, fl * 128:(fl + 1) * 128], AF.Identity, [cst_b],
                            [pb[bank], tmb], scale=C("glg", j * 16 + fc))
                    tt(tm, tm, bias_f[:, 4 * g * 128:(4 * g + 4) * 128], ALU.add, [bias_b], [tmb])
                    tt(uT[:, 4 * g:4 * g + 4, blk], tm.rearrange("p (f i) -> p f i", f=4), uT[:, 4 * g:4 * g + 4, blk],
                       ALU.mult, [tmb], [ub[4 * g + i] for i in range(4)])
            out_proj(li, uT, ub)

        def ffn(li):
            ar = Arena()
            g_f, _ = ar.bf16(44 * T)
            g_sb = g_f.rearrange("p (k t) -> p k t", k=44)
            gb = [Buf(ar.bar) for _ in range(44)]
            ya = [ar.f32(T) for _ in range(2)]
            yb = [ar.f32(T) for _ in range(2)]
            sa = [ar.f32(T) for _ in range(2)]
            corr_f, corr_b = ar.f32(176)
            corr = corr_f.rearrange("p (c t) -> p c t", t=2)
            t88, t88_b = ar.f32(88)
            w0 = C("fdw", (li * 3 + 0) * 88, 88)
            w1 = C("fdw", (li * 3 + 1) * 88, 88)
            tt(corr[:, :, 1], w0, fh[:, li, :, 1], ALU.mult, [cst_b, fh_b[li]], [corr_b])
            tt(t88, w0, fh[:, li, :, 0], ALU.mult, [cst_b, fh_b[li]], [t88_b])
            tt(corr[:, :, 0], w1, fh[:, li, :, 1], ALU.mult, [cst_b, fh_b[li]], [corr_b])
            tt(corr[:, :, 0], corr[:, :, 0], t88, ALU.add, [t88_b], [corr_b])
            for i in range(22):
                slot, sbuf = w_acquire((li, "up", i))
                sv = slot[:].rearrange("p (k c) -> p k c", k=KC)
                for cc in range(2):
                    ci = 2 * i + cc
                    bA, bB = nbank(), nbank()
                    proj_fm(sv, sbuf, 2 * cc, bA, lambda kc: h_sb[:, kc, :], lambda kc: [hb[kc]])
                    proj_fm(sv, sbuf, 2 * cc + 1, bB, lambda kc: h_sb[:, kc, :], lambda kc: [hb[kc]])
                    ys = (ya[ci % 2], yb[ci % 2])
                    for (bank, ch, (y, y_b)) in ((bA, ci, ys[0]), (bB, 44 + ci, ys[1])):
                        act(y, ps[bank][:], AF.Identity, [cst_b], [pb[bank], y_b],
                            bias=C("fdb", li * 88 + ch), scale=C("fdw", (li * 3 + 2) * 88 + ch))
                        P.op("act", lambda e, bank=bank, ch=ch: e.copy(out=fh[:, li, ch, :], in_=ps[bank][:, T - 2:T]),
                             reads=[corr_b], writes=[pb[bank], fh_b[li]])
                        stt(y[:, 1:T], ps[bank][:, 0:T - 1], C("fdw", (li * 3 + 1) * 88 + ch), y[:, 1:T], ALU.mult, ALU.add,
                            [], [pb[bank], y_b])
                        stt(y[:, 2:T], ps[bank][:, 0:T - 2], C("fdw", (li * 3 + 0) * 88 + ch), y[:, 2:T], ALU.mult, ALU.add,
                            [], [pb[bank], y_b])
                        tt(y[:, 0:2], y[:, 0:2], corr[:, ch, :], ALU.add, [corr_b], [y_b])
                    s, s_b = sa[ci % 2]
                    act(s, ys[0][0], AF.Silu, [ys[0][1]], [s_b])
                    tt(g_sb[:, ci, :], s, ys[1][0], ALU.mult, [s_b, ys[1][1]], [gb[ci]])
                w_release()
            for m in range(16):
                slot, sbuf = w_acquire((li, "down", m))
                bank = nbank()
                for kc in range(44):
                    mm(ps[bank][:], slot[:, kc * 128:(kc + 1) * 128], g_sb[:, kc, :], kc == 0, kc == 43, [sbuf[kc // 22], gb[kc]], bank)
                tt(x_sb[:, m, :], x_sb[:, m, :], ps[bank][:], ALU.add, [], [xb[m], pb[bank]])
                w_release()

        last_store = None
        for ti in range(n_tiles):
            seq_first = (ti % TPS == 0)
            tok0 = ti * T
            pos0 = (ti % TPS) * T
            for qd in range(4):
                P.op("sp", lambda e, tok0=tok0, qd=qd: e.dma_start(out=x_sb[:, 4 * qd:4 * qd + 4, :],
                                                                 in_=xT_v[:, 4 * qd:4 * qd + 4, tok0:tok0 + T]),
                     writes=xb[4 * qd:4 * qd + 4], sem=x_sems[qd], inc=16)
            P.op("sp", lambda e, pos0=pos0: e.dma_start(out=cos_sb[:], in_=rope_d[0, :, pos0:pos0 + T]),
                 writes=[cos_b], sem=cos_sem, inc=16)
            P.op("sp", lambda e, pos0=pos0: e.dma_start(out=sin_sb[:], in_=rope_d[1, :, pos0:pos0 + T]),
                 writes=[sin_b], sem=sin_sem, inc=16)
            if seq_first:
                P.op("dve", lambda e: e.memset(S_f32[:], 0.0), writes=[b for r in Sf_b for b in r])
                P.op("dve", lambda e: e.memset(S_bf[:], 0.0), writes=Sb_b)
                P.op("dve", lambda e: e.memset(uh[:], 0.0), writes=[b for r in uh_b for b in r])
                P.op("dve", lambda e: e.memset(fh[:], 0.0), writes=fh_b)
            for li in layers:
                rmsnorm("mixg", li)
                if li % 2 == 0:
                    if not seq_first:
                        for hh in range(4):
                            P.op("act", lambda e, hh=hh, li=li: e.copy(out=S_bf[:, hh, :], in_=S_f32[:, li // 2, hh, :]),
                                 reads=[Sf_b[li // 2][hh]], writes=[Sb_b[hh]])
                    else:
                        P.op("dve", lambda e: e.memset(S_bf[:], 0.0), writes=Sb_b)
                    even_mixer(li)
                else:
                    odd_mixer(li)
                if do_ffn:
                    rmsnorm("ffng", li)
                    ffn(li)
            if final_norm:
                stg, stg_b = rmsnorm("fing", 0, in_place=True)
                last_store = P.op("sp", lambda e, tok0=tok0, stg=stg: e.dma_start(
                    out=outT_v[:, :, tok0:tok0 + T], in_=stg.rearrange("p (k t) -> p k t", k=KC)),
                    reads=[stg_b], sem=o_sem, inc=16)
                extra_bar[:] = [last_store]
            else:
                last_store = P.op("sp", lambda e, tok0=tok0: e.dma_start(out=outT_v[:, :, tok0:tok0 + T], in_=x_sb[:]),
                                  reads=xb, sem=o_sem, inc=16)
        P.wait_all("sp", [last_store, cos_b.last_write, sin_b.last_write, cst_b.last_write])
        P.finish()
        build_program.stats = (P.n_ops, P.n_wait)
    return nc


def _pm(v, c):
    v = np.asarray(v, np.float32)
    lead = v.shape[:-1]
    return np.moveaxis(v.reshape(lead + (c, 128)), -1, 0)


def _col_piece(W, cols):
    blk = W[:, cols]
    return blk.reshape(16, 128, 512).transpose(1, 0, 2).reshape(128, 8192)


def _consts():
    H = 4
    hh = np.arange(H, dtype=np.float64)
    gamma = 1.0 - 2.0 ** (-5.0 - hh)
    pos = np.arange(128, dtype=np.float64)
    n = pos[None, :]
    m = pos[:, None]
    cm = (m // 64)
    cn = (n // 64)
    maskT = np.zeros((128, H, 128))
    for h in range(H):
        same = gamma[h] ** np.abs(n - m)
        caus = gamma[h] ** (n - m)
        maskT[:, h, :] = np.where(cm == cn, same, np.where(cn > cm, caus, 0.0)) / 16.0
    zeta = np.stack([gamma[h] ** (127.0 - pos) / 16.0 for h in range(H)], axis=1)
    xi = np.broadcast_to(np.stack([gamma[h] ** (pos + 1.0) for h in range(H)], axis=0)[None], (128, H, 128))
    ci = np.arange(128) // 64
    wmask = (ci[:, None] <= ci[None, :]).astype(np.float32)
    inv = (10000.0 ** (-np.arange(0, 256, 2, dtype=np.float32) / 256.0)).astype(np.float32)
    ang = (np.arange(SEQ, dtype=np.float32)[:, None] * inv[None, :]).astype(np.float32)
    rope = np.stack([np.cos(ang).T, np.sin(ang).T]).astype(np.float32)
    return maskT.astype(np.float32), zeta.astype(np.float32), np.ascontiguousarray(xi, dtype=np.float32), wmask, rope


def prep_inputs(inp):
    f = lambda k: np.asarray(inp[k], np.float32)
    maskT, zeta, xi, wmask, rope = _consts()
    cst = np.zeros((128, NCST), np.float32)

    def put(name, arr):
        a = np.ascontiguousarray(arr, dtype=np.float32).reshape(128, -1)
        o = CST_OFF[name]
        cst[:, o:o + a.shape[1]] = a

    put("mixg", _pm(f("mix_norm_g"), 16))
    put("ffng", _pm(f("ffn_norm_g"), 16))
    put("fing", _pm(f("final_norm_g"), 16))
    put("fdw", _pm(f("ffn_dw_w"), 88))
    put("fdb", _pm(f("ffn_dw_b"), 88))
    put("cdw", _pm(f("ev_conv_dw_w"), 8))
    put("cdb", _pm(f("ev_conv_dw_b"), 8))
    put("clg", _pm(f("ev_conv_ln_g"), 8))
    put("clb", _pm(f("ev_conv_ln_b"), 8))
    put("rlg", _pm(f("ev_ret_ln_g"), 8))
    put("glg", _pm(f("od_gm_ln_g"), 16))
    put("glb", _pm(f("od_gm_ln_b"), 16))
    put("zeta", zeta)
    put("maskT", maskT)
    put("xi", xi)
    put("bsbc", np.broadcast_to(f("od_gm_bs")[None], (128, 2, 4, 128)))
    put("wmask", wmask)
    put("ident", np.eye(128, dtype=np.float32))
    wsT = np.ascontiguousarray(np.transpose(f("od_gm_ws"), (3, 0, 1, 2))).reshape(128, 1024)

    wstream = np.zeros((NPIECE, 128, 8192), np.float32)
    ar = np.arange
    for li in range(4):
        j = li // 2
        base = PIECE_BASE[li]
        for pi, kd in enumerate(layer_pieces(li)):
            dst = wstream[base + pi]
            k0, k1 = kd
            if k0 == "glu":
                W = f("ev_w_in")[j]
                c0 = k1 * 256
                dst[:] = _col_piece(W, np.concatenate([ar(c0, c0 + 128), 1024 + ar(c0, c0 + 128),
                                                       ar(c0 + 128, c0 + 256), 1024 + ar(c0 + 128, c0 + 256)]))
            elif k0 in ("q", "k", "v", "gate"):
                W = f("ev_w_in")[j]
                off = {"q": 2048, "k": 3072, "v": 4096, "gate": 5120}[k0]
                dst[:] = _col_piece(W, off + ar(k1 * 512, (k1 + 1) * 512))
            elif k0 == "mout":
                W = f("ev_w_out")[j] if li % 2 == 0 else f("od_w_out")[j]
                dst[:] = _col_piece(W, ar(k1 * 512, (k1 + 1) * 512))
            elif k0 == "gu":
                dst[:] = _col_piece(f("od_w_in")[j], ar(k1 * 512, (k1 + 1) * 512))
            elif k0 == "gv":
                dst[:] = _col_piece(f("od_w_in")[j], 2048 + ar(k1 * 512, (k1 + 1) * 512))
            elif k0 == "up":
                W = f("ffn_w_up")[li]
                c0 = k1 * 256
                dst[:] = _col_piece(W, np.concatenate([ar(c0, c0 + 128), DFF + ar(c0, c0 + 128),
                                                       ar(c0 + 128, c0 + 256), DFF + ar(c0 + 128, c0 + 256)]))
            elif k0 == "down":
                Wd = f("ffn_w_down")[li]
                blk = Wd[:, k1 * 128:(k1 + 1) * 128]
                dst[:, :5632] = blk.reshape(44, 128, 128).transpose(1, 0, 2).reshape(128, 5632)
    return {"wstream": wstream, "cst": cst, "wsT": wsT, "rope": rope}


_PROG_CACHE = {}


def kernel(**inputs):
    x = np.asarray(inputs["x"], np.float32)
    shared = prep_inputs(inputs)
    if "nc" not in _PROG_CACHE:
        _PROG_CACHE["nc"] = build_program()
    nc = _PROG_CACHE["nc"]
    in_maps = []
    for c in range(NCORE):
        xc = x[c * NSEQ:(c + 1) * NSEQ].reshape(NSEQ * SEQ, D)
        m = dict(shared)
        m["xT"] = np.ascontiguousarray(xc.T)
        in_maps.append(m)
    res = run_bass_kernel_spmd(nc, in_maps, core_ids=list(range(NCORE)))
    out = np.empty((NCORE * NSEQ, SEQ, D), np.float32)
    for c in range(NCORE):
        out[c * NSEQ:(c + 1) * NSEQ] = np.asarray(res.results[c]["outT"]).T.reshape(NSEQ, SEQ, D)
    return out
```

```python
import numpy as np
from contextlib import ExitStack
import concourse.bass as bass
import concourse.mybir as mybir
from concourse.bass_utils import run_bass_kernel_spmd

F32 = mybir.dt.float32
BF16 = mybir.dt.bfloat16
AF = mybir.ActivationFunctionType
ALU = mybir.AluOpType

D = 2048
KC = 16
T = 512
NB = T // 128
SEQ = 2048
TPS = SEQ // T
NSEQ = 2
NCORE = 8
DFF = 5632
NSLOT = 2
RMS_EPS = 1e-6
LN_EPS = 1e-5
ARENA_W = 72 * 256


class Buf:
    __slots__ = ("last_write", "readers")

    def __init__(self, readers=None):
        self.last_write = None
        self.readers = list(readers) if readers else []


class Prog:
    ENGS = ("pe", "act", "dve", "pool", "sp")

    def __init__(self, nc, stack):
        self.nc = nc
        self.stack = stack
        self.ops = {e: [] for e in self.ENGS}
        self.sem = {}
        self.sems = []
        self.semval = []
        for e in self.ENGS:
            self.sem[e] = self.new_sem("s_" + e)
        self.seen = {e: {} for e in self.ENGS}
        self.n_ops = 0
        self.n_wait = 0

    def new_sem(self, name):
        h = self.stack.enter_context(self.nc.semaphore(name))
        self.sems.append(h)
        self.semval.append(0)
        return len(self.sems) - 1

    def barrier(self):
        return [(self.sem[e], self.semval[self.sem[e]]) for e in ("pe", "act", "dve") if self.semval[self.sem[e]] > 0]

    def op(self, eng, fn, reads=(), writes=(), deps=(), sem=None, inc=1, mark=True):
        need = {}

        def add(tok):
            if tok is None:
                return
            s, v = tok
            if need.get(s, 0) < v:
                need[s] = v

        for b in reads:
            add(b.last_write)
        for b in writes:
            add(b.last_write)
            for t in b.readers:
                add(t)
        for t in deps:
            add(t)
        waits = []
        seen = self.seen[eng]
        own = self.sem[eng]
        for s, v in need.items():
            if s == own and eng == "pe":
                continue
            if seen.get(s, 0) >= v:
                continue
            seen[s] = v
            waits.append((s, v))
        if mark:
            si = own if sem is None else sem
            self.semval[si] += inc
            tok = (si, self.semval[si])
            inc_si = si
        else:
            tok = (own, self.semval[own] + 1)
            inc_si = None
        sems = self.sems
        self.n_ops += 1
        self.n_wait += len(waits)

        def emit(e, waits=waits, fn=fn, inc_si=inc_si, inc=inc):
            for s, v in waits:
                e.wait_ge(sems[s], v)
            ins = fn(e)
            if inc_si is not None:
                ins.then_inc(sems[inc_si], inc)

        self.ops[eng].append(emit)
        for b in reads:
            b.readers.append(tok)
            if len(b.readers) > 48:
                mx = {}
                for t in b.readers:
                    if mx.get(t[0], 0) < t[1]:
                        mx[t[0]] = t[1]
                b.readers = list(mx.items())
        for b in writes:
            b.last_write = tok
            b.readers = []
        return tok

    def wait_all(self, eng, toks):
        sems = self.sems
        toks = [t for t in toks if t is not None]

        def emit(e):
            for s, v in toks:
                e.wait_ge(sems[s], v)

        self.ops[eng].append(emit)

    def finish(self):
        with self.nc.Block() as block:
            @block.tensor
            def _(e):
                for f in self.ops["pe"]:
                    f(e)

            @block.scalar
            def _(e):
                for f in self.ops["act"]:
                    f(e)

            @block.vector
            def _(e):
                for f in self.ops["dve"]:
                    f(e)

            @block.gpsimd
            def _(e):
                for f in self.ops["pool"]:
                    f(e)

            @block.sync
            def _(e):
                for f in self.ops["sp"]:
                    f(e)


CST_SPEC = [
    ("mixg", 4 * 16), ("ffng", 4 * 16), ("fing", 16),
    ("fdw", 4 * 3 * 88), ("fdb", 4 * 88),
    ("cdw", 2 * 31 * 8), ("cdb", 2 * 8), ("clg", 2 * 8), ("clb", 2 * 8), ("rlg", 2 * 8),
    ("glg", 2 * 16), ("glb", 2 * 16),
    ("zeta", 4), ("maskT", 4 * 128), ("xi", 4 * 128), ("bsbc", 2 * 4 * 128),
    ("wmask", 128), ("ident", 128),
]
CST_OFF = {}
_o = 0
for _n, _w in CST_SPEC:
    CST_OFF[_n] = _o
    _o += _w
NCST = _o


def layer_pieces(li):
    out = []
    if li % 2 == 0:
        out += [("glu", gp) for gp in range(4)]
        for hp in range(2):
            out += [("q", hp), ("k", hp), ("v", hp), ("gate", hp)]
        out += [("mout", m) for m in range(4)]
    else:
        out += [("gv", m) for m in range(4)]
        out += [("gu", m) for m in range(4)]
        out += [("mout", m) for m in range(4)]
    out += [("up", i) for i in range(22)]
    out += [("down", m) for m in range(16)]
    return out


PIECE_BASE = []
_b = 0
for _li in range(4):
    PIECE_BASE.append(_b)
    _b += len(layer_pieces(_li))
NPIECE = _b


def build_program(n_tiles=NSEQ * TPS, layers=(0, 1, 2, 3), do_ffn=True, final_norm=True):
    nc = bass.Bass("TRN2", target_bir_lowering=False)
    xT = nc.dram_tensor("xT", [D, NSEQ * SEQ], F32, kind="ExternalInput").ap()
    wst = nc.dram_tensor("wstream", [NPIECE, 128, 8192], F32, kind="ExternalInput").ap()
    cst_d = nc.dram_tensor("cst", [128, NCST], F32, kind="ExternalInput").ap()
    wsT_d = nc.dram_tensor("wsT", [128, 1024], F32, kind="ExternalInput").ap()
    rope_d = nc.dram_tensor("rope", [2, 128, SEQ], F32, kind="ExternalInput").ap()
    outT = nc.dram_tensor("outT", [D, NSEQ * SEQ], F32, kind="ExternalOutput").ap()
    xT_v = xT.rearrange("(k p) t -> p k t", p=128)
    outT_v = outT.rearrange("(k p) t -> p k t", p=128)

    with ExitStack() as st:
        P = Prog(nc, st)
        sb = lambda name, shape, dt: st.enter_context(nc.sbuf_tensor(name, shape, dt))
        x_sb = sb("x_sb", [128, KC, T], F32)
        h_sb = sb("h_sb", [128, KC, T], BF16)
        slots = [sb(f"wslot{i}", [128, 8192], BF16) for i in range(NSLOT)]
        cst = sb("cst_sb", [128, NCST], F32)
        cos_sb = sb("cos_sb", [128, T], F32)
        sin_sb = sb("sin_sb", [128, T], F32)
        S_f32 = sb("S_f32", [128, 2, 4, 512], F32)
        S_bf = sb("S_bf", [128, 4, 512], BF16)
        uh = sb("uh", [128, 2, 8, 30], F32)
        fh = sb("fh", [128, 4, 88, 2], F32)
        ident = sb("ident", [128, 128], BF16)
        ones = sb("ones", [128, 128], BF16)
        wmT = sb("wmT", [128, 2, 4, 128], BF16)
        rowsum = sb("rowsum", [128, 2, 4, 128], F32)
        arena = sb("arena", [128, ARENA_W], F32)
        ps = [st.enter_context(nc.psum_tensor(f"ps{i}", [128, 512], F32)) for i in range(8)]

        xb = [Buf() for _ in range(KC)]
        hb = [Buf() for _ in range(KC)]
        pb = [Buf() for _ in range(8)]
        slot_b = [[Buf(), Buf()] for _ in range(NSLOT)]
        slot_sem = [[P.new_sem(f"slot{i}_{h}") for h in range(2)] for i in range(NSLOT)]
        cst_b = Buf()
        cst_sem = P.new_sem("cst")
        cos_b, sin_b = Buf(), Buf()
        cos_sem, sin_sem = P.new_sem("cos"), P.new_sem("sin")
        x_sems = [P.new_sem(f"xld{i}") for i in range(4)]
        o_sem = P.new_sem("ost")
        Sf_b = [[Buf() for _ in range(4)] for _ in range(2)]
        Sb_b = [Buf() for _ in range(4)]
        uh_b = [[Buf() for _ in range(8)] for _ in range(2)]
        fh_b = [Buf() for _ in range(4)]
        misc_b = Buf()

        def C(name, idx=0, n=1):
            o = CST_OFF[name] + idx
            return cst[:, o:o + n]

        extra_bar = []
        class Arena:
            def __init__(self, start=0):
                self.off = start
                self.bar = P.barrier() + list(extra_bar)

            def f32(self, n):
                a = arena[:, self.off:self.off + n]
                self.off += n
                assert self.off <= ARENA_W, self.off
                return a, Buf(self.bar)

            def bf16(self, n):
                w = (n + 1) // 2
                a = arena[:, self.off:self.off + w].bitcast(BF16)
                self.off += w
                assert self.off <= ARENA_W, self.off
                return a, Buf(self.bar)

        pass_pieces = []
        for li in layers:
            kinds = layer_pieces(li)
            for i, kd in enumerate(kinds):
                if not do_ffn and kd[0] in ("up", "down"):
                    continue
                pass_pieces.append((PIECE_BASE[li] + i, 5632 if kd[0] == "down" else 8192, (li,) + kd))
        total_pieces = n_tiles * len(pass_pieces)
        wstate = {"load": 0, "use": 0}
        slot_cls = [None] * NSLOT

        def w_load():
            p = wstate["load"]
            if p >= total_pieces:
                return
            hidx, n, kd = pass_pieces[p % len(pass_pieces)]
            s = p % NSLOT
            kind = kd[1]
            cls = "down" if kind == "down" else ("mov" if kind in ("v", "gate", "gv") else "col")
            xdeps = []
            if slot_cls[s] is not None and slot_cls[s] != cls:
                xdeps = list(slot_b[s][1].readers) + [slot_b[s][1].last_write]
            slot_cls[s] = cls
            for hf in range(2):
                if kind == "down":
                    o_ap = slots[s][:, hf * 2816:(hf + 1) * 2816]
                    i_ap = wst[hidx, :, hf * 2816:(hf + 1) * 2816]
                elif kind in ("v", "gate", "gv"):
                    o_ap = slots[s][:, hf * 4096:(hf + 1) * 4096]
                    i_ap = wst[hidx, :, hf * 4096:(hf + 1) * 4096]
                else:
                    o_ap = slots[s][:].rearrange("p (k c) -> p k c", k=KC)[:, :, hf * 256:(hf + 1) * 256]
                    i_ap = wst[hidx].rearrange("p (k c) -> p k c", k=KC)[:, :, hf * 256:(hf + 1) * 256]
                P.op("pool", lambda e, o_ap=o_ap, i_ap=i_ap: e.dma_start(out=o_ap, in_=i_ap),
                     writes=[slot_b[s][hf]], deps=(xdeps if hf == 0 else ()), sem=slot_sem[s][hf], inc=16)
            wstate["load"] += 1

        def w_acquire(expect):
            p = wstate["use"]
            _, _, kd = pass_pieces[p % len(pass_pieces)]
            assert kd == expect, (kd, expect)
            s = p % NSLOT
            return slots[s], slot_b[s]

        def w_release():
            wstate["use"] += 1
            w_load()

        bank_rr = {"i": 0}

        def nbank():
            b = bank_rr["i"]
            bank_rr["i"] = (b + 1) % 6
            return b

        def mm(out, lhsT, rhs, start, stop, reads, bank, mark=None):
            if mark is None:
                mark = stop
            return P.op("pe", lambda e: e.matmul(out, lhsT, rhs, start=start, stop=stop),
                        reads=reads, writes=[pb[bank]], mark=mark)

        def act(out, in_, func, reads, writes, bias=None, scale=None):
            kw = {}
            if bias is not None:
                kw["bias"] = bias
            if scale is not None:
                kw["scale"] = scale
            return P.op("act", lambda e: e.activation(out, in_, func, **kw), reads=reads, writes=writes)

        def tt(out, in0, in1, op, reads, writes):
            return P.op("dve", lambda e: e.tensor_tensor(out, in0, in1, op), reads=reads, writes=writes)

        def stt(out, in0, scalar, in1, op0, op1, reads, writes):
            return P.op("dve", lambda e: e.scalar_tensor_tensor(out, in0, scalar, in1, op0, op1), reads=reads, writes=writes)

        def ts(out, in0, s1, s2, op0, op1, reads, writes):
            return P.op("dve", lambda e: e.tensor_scalar(out, in0, s1, s2, op0, op1), reads=reads, writes=writes)

        def recip(out, in_, reads, writes):
            return P.op("dve", lambda e: e.reciprocal(out, in_), reads=reads, writes=writes)

        P.op("sp", lambda e: e.dma_start(out=cst[:], in_=cst_d), writes=[cst_b], sem=cst_sem, inc=16)
        for _ in range(NSLOT):
            w_load()
        ar = Arena()
        wtmp, wtmp_b = ar.f32(1024)
        wt_sem = P.new_sem("wtmp")
        P.op("sp", lambda e: e.dma_start(out=wtmp, in_=wsT_d), writes=[wtmp_b], sem=wt_sem, inc=16)
        P.op("dve", lambda e: e.memset(ones[:], 1.0), writes=[misc_b])
        P.op("act", lambda e: e.copy(out=ident[:], in_=C("ident", 0, 128)), reads=[cst_b], writes=[misc_b])
        tt(wmT[:].rearrange("p j g i -> p (j g) i"), wtmp.rearrange("p (a i) -> p a i", a=8),
           C("wmask", 0, 128).unsqueeze(1).to_broadcast([128, 8, 128]), ALU.mult, [wtmp_b, cst_b], [misc_b])
        for j in range(2):
            mm(ps[j][:], ones[:], wmT[:, j].rearrange("p g i -> p (g i)"), True, True, [misc_b], j)
            P.op("act", lambda e, j=j: e.copy(out=rowsum[:, j].rearrange("p g i -> p (g i)"), in_=ps[j][:]),
                 reads=[], writes=[pb[j], misc_b])

        def rmsnorm(gname, gidx, in_place=False):
            ar = Arena()
            sq = [ar.bf16(T) for _ in range(2)]
            sd, sd_b = ar.f32(T)
            rstd, rstd_b = ar.f32(T)
            for kc in range(KC):
                a, ab = sq[kc % 2]
                act(a, x_sb[:, kc, :], AF.Square, [xb[kc]], [ab])
                mm(ps[6][:], ones[:], a, kc == 0, kc == KC - 1, [ab, misc_b], 6, mark=True)
            act(sd, ps[6][:], AF.Sqrt, [], [pb[6], sd_b], bias=RMS_EPS, scale=1.0 / D)
            recip(rstd, sd, [sd_b], [rstd_b])
            if in_place:
                stg, stg_b = ar.f32(KC * T)
            for kc in range(KC):
                if in_place:
                    stt(stg[:, kc * T:(kc + 1) * T], x_sb[:, kc, :], C(gname, gidx * 16 + kc), rstd, ALU.mult, ALU.mult,
                        [rstd_b, cst_b, xb[kc]], [stg_b])
                else:
                    stt(h_sb[:, kc, :], x_sb[:, kc, :], C(gname, gidx * 16 + kc), rstd, ALU.mult, ALU.mult,
                        [rstd_b, cst_b, xb[kc]], [hb[kc]])
            if in_place:
                return stg, stg_b

        def proj_fm(sv, sbuf, m, bank, rhs_of, rbufs_of, nk=KC):
            for kc in range(nk):
                mm(ps[bank][:], sv[:, kc, m * 128:(m + 1) * 128], rhs_of(kc), kc == 0, kc == nk - 1,
                   [sbuf[m // 2]] + rbufs_of(kc), bank)

        def out_proj(li, cat, catb):
            for mp in range(4):
                slot, sbuf = w_acquire((li, "mout", mp))
                sv = slot[:].rearrange("p (k c) -> p k c", k=KC)
                for m in range(4):
                    mo = mp * 4 + m
                    bank = nbank()
                    proj_fm(sv, sbuf, m, bank, lambda kc: cat[:, kc, :], lambda kc: [catb[kc]])
                    tt(x_sb[:, mo, :], x_sb[:, mo, :], ps[bank][:], ALU.add, [], [xb[mo], pb[bank]])
                w_release()

        def even_mixer(li):
            j = li // 2
            ar = Arena()
            cat_f, _ = ar.bf16(KC * T)
            cat = cat_f.rearrange("p (k t) -> p k t", k=KC)
            catb = [Buf(ar.bar) for _ in range(KC)]
            cat_end = ar.off
            c_f, _ = ar.f32(8 * T)
            c_sb = c_f.rearrange("p (k t) -> p k t", k=8)
            cb = [Buf(ar.bar) for _ in range(8)]
            ub_f, _ = ar.bf16(8 * (32 + T))
            u_bf = ub_f.rearrange("p (k t) -> p k t", k=8)
            uib = [Buf(ar.bar) for _ in range(8)]
            tb = [ar.bf16(T) for _ in range(4)]
            ln_off = ar.off
            sig = [ar.f32(T) for _ in range(2)]
            dg = []
            for _ in range(2):
                d_f, d_b = ar.bf16(31 * 128)
                dg.append((d_f.rearrange("p (k i) -> p k i", k=31), d_b))
            def glu_chunk(sv, sbuf, gp, cc):
                c = 2 * gp + cc
                bA, bB = nbank(), nbank()
                proj_fm(sv, sbuf, 2 * cc, bA, lambda kc: h_sb[:, kc, :], lambda kc: [hb[kc]])
                proj_fm(sv, sbuf, 2 * cc + 1, bB, lambda kc: h_sb[:, kc, :], lambda kc: [hb[kc]])
                sg, sgb = sig[c % 2]
                act(sg, ps[bB][:], AF.Sigmoid, [], [pb[bB], sgb])
                P.op("act", lambda e, c=c: e.copy(out=u_bf[:, c, 0:30], in_=uh[:, j, c, :]),
                     reads=[uh_b[j][c]], writes=[uib[c]])
                tt(u_bf[:, c, 30:30 + T], ps[bA][:], sg, ALU.mult, [sgb], [pb[bA], uib[c]])
                P.op("act", lambda e, c=c: e.copy(out=uh[:, j, c, :], in_=u_bf[:, c, T:T + 30]),
                     reads=[uib[c]], writes=[uh_b[j][c]])
                dgc, dgb = dg[c % 2]
                o = CST_OFF["cdw"] + j * 31 * 8 + c
                wk = cst[:, o:o + 31 * 8:8]
                tt(dgc, ident[:].unsqueeze(1).to_broadcast([128, 31, 128]),
                   wk.unsqueeze(2).to_broadcast([128, 31, 128]), ALU.mult, [misc_b, cst_b], [dgb])

            def conv_chunk(c):
                dgc, dgb = dg[c % 2]
                bank = nbank()
                for k in range(31):
                    mm(ps[bank][:], dgc[:, k, :], u_bf[:, c, k:k + T], k == 0, k == 30, [dgb, uib[c]], bank)
                a, ab = tb[(2 * c) % 4]
                q, qb = tb[(2 * c + 1) % 4]
                bia = C("cdb", j * 8 + c)
                act(c_sb[:, c, :], ps[bank][:], AF.Identity, [cst_b], [pb[bank], cb[c]], bias=bia, scale=1.0)
                act(a, ps[bank][:], AF.Identity, [cst_b], [pb[bank], ab], bias=bia, scale=1.0)
                act(q, ps[bank][:], AF.Square, [cst_b], [pb[bank], qb], bias=bia, scale=1.0)
                def stat_mm(a=a, ab=ab, q=q, qb=qb, c=c):
                    mm(ps[6][:], ones[:], a, c == 0, c == 7, [ab, misc_b], 6, mark=True)
                    mm(ps[7][:], ones[:], q, c == 0, c == 7, [qb, misc_b], 7, mark=True)
                if pstat[0] is not None:
                    pstat[0]()
                pstat[0] = stat_mm

            pstat = [None]
            pend = None
            for gp in range(4):
                slot, sbuf = w_acquire((li, "glu", gp))
                sv = slot[:].rearrange("p (k c) -> p k c", k=KC)
                for cc in range(2):
                    glu_chunk(sv, sbuf, gp, cc)
                    if pend is not None:
                        conv_chunk(pend)
                    pend = 2 * gp + cc
                w_release()
            conv_chunk(pend)
            pstat[0]()
            ar = Arena(ln_off)
            mean, mean_b = ar.f32(T)
            msq, msq_b = ar.f32(T)
            rstd, rstd_b = ar.f32(T)
            act(mean, ps[6][:], AF.Identity, [], [pb[6], mean_b], scale=1.0 / 1024)
            tt(msq, mean, mean, ALU.mult, [mean_b], [msq_b])
            stt(msq, ps[7][:], 1.0 / 1024, msq, ALU.mult, ALU.subtract, [], [pb[7], msq_b])
            act(msq, msq, AF.Sqrt, [], [msq_b], bias=LN_EPS, scale=1.0)
            recip(rstd, msq, [msq_b], [rstd_b])
            for c in range(8):
                tt(c_sb[:, c, :], c_sb[:, c, :], mean, ALU.subtract, [mean_b], [cb[c]])
                tt(c_sb[:, c, :], c_sb[:, c, :], rstd, ALU.mult, [rstd_b], [cb[c]])
                act(cat[:, c, :], c_sb[:, c, :], AF.Silu, [cb[c], cst_b], [catb[c]],
                    bias=C("clb", j * 8 + c), scale=C("clg", j * 8 + c))
            ar = Arena(cat_end)
            qT_f, qT_b = ar.bf16(4 * T)
            kT_f, kT_b = ar.bf16(4 * T)
            qx_f, qx_b = ar.bf16(4 * T)
            qT = qT_f.rearrange("p (k t) -> p k t", k=4)
            kT = kT_f.rearrange("p (k t) -> p k t", k=4)
            qx = qx_f.rearrange("p (k t) -> p k t", k=4)
            v_f, v_b = ar.bf16(NB * 512)
            sg_f, sg_b = ar.bf16(NB * 512)
            kz_f, kz_b = ar.bf16(NB * 512)
            v_sb = v_f.rearrange("p (b e) -> p b e", b=NB)
            sg_sb = sg_f.rearrange("p (b e) -> p b e", b=NB)
            kz = kz_f.rearrange("p (b e) -> p b e", b=NB)
            t1, t1_b = ar.f32(T)
            t2, t2_b = ar.f32(T)
            sc = [ar.bf16(128) for _ in range(8)]
            on = [ar.f32(256) for _ in range(4)]
            bo = [ar.bf16(256) for _ in range(4)]
            stats = [ar.f32(8) for _ in range(4)]
            mv = [ar.f32(4) for _ in range(4)]
            for hp in range(2):
                for nm, dst, dst_b in (("q", qT, qT_b), ("k", kT, kT_b)):
                    slot, sbuf = w_acquire((li, nm, hp))
                    sv = slot[:].rearrange("p (k c) -> p k c", k=KC)
                    for hl in range(2):
                        bA, bB = nbank(), nbank()
                        proj_fm(sv, sbuf, 2 * hl, bA, lambda kc: h_sb[:, kc, :], lambda kc: [hb[kc]])
                        proj_fm(sv, sbuf, 2 * hl + 1, bB, lambda kc: h_sb[:, kc, :], lambda kc: [hb[kc]])
                        tt(t1, ps[bA][:], cos_sb[:], ALU.mult, [cos_b], [pb[bA], t1_b])
                        tt(t2, ps[bB][:], sin_sb[:], ALU.mult, [sin_b], [pb[bB], t2_b])
                        tt(dst[:, 2 * hl, :], t1, t2, ALU.subtract, [t1_b, t2_b], [dst_b])
                        tt(t1, ps[bA][:], sin_sb[:], ALU.mult, [sin_b], [pb[bA], t1_b])
                        tt(t2, ps[bB][:], cos_sb[:], ALU.mult, [cos_b], [pb[bB], t2_b])
                        tt(dst[:, 2 * hl + 1, :], t1, t2, ALU.add, [t1_b, t2_b], [dst_b])
                    w_release()
                    if nm == "q":
                        for hl in range(2):
                            hh = 2 * hp + hl
                            for dc in range(2):
                                tt(qx[:, 2 * hl + dc, :].rearrange("p (b n) -> p b n", b=NB),
                                   qT[:, 2 * hl + dc, :].rearrange("p (b n) -> p b n", b=NB),
                                   C("xi", hh * 128, 128).unsqueeze(1).to_broadcast([128, NB, 128]),
                                   ALU.mult, [qT_b, cst_b], [qx_b])
                    else:
                        for hl in range(2):
                            hh = 2 * hp + hl
                            for b in range(NB):
                                bank = nbank()
                                pv = ps[bank][:, 0:128].bitcast(BF16)
                                for dc in range(2):
                                    P.op("pe", lambda e, pv=pv, dc=dc, hl=hl, b=b: e.transpose(
                                        pv[:, dc * 128:(dc + 1) * 128], kT[:, 2 * hl + dc, b * 128:(b + 1) * 128], ident[:]),
                                        reads=[kT_b, misc_b], writes=[pb[bank]])
                                act(kz[:, b, hl * 256:(hl + 1) * 256], pv, AF.Identity, [cst_b], [pb[bank], kz_b],
                                    scale=C("zeta", hh))
                for nm, dst, dst_b, fn in (("v", v_sb, v_b, AF.Identity), ("gate", sg_sb, sg_b, AF.Silu)):
                    slot, sbuf = w_acquire((li, nm, hp))
                    sv = slot[:].rearrange("p (k c) -> p k c", k=KC)
                    for b in range(NB):
                        bank = nbank()
                        for kc in range(KC):
                            mm(ps[bank][:], h_sb[:, kc, b * 128:(b + 1) * 128], sv[:, kc, :], kc == 0, kc == KC - 1,
                               [sbuf[kc // 8], hb[kc]], bank)
                        act(dst[:, b, :], ps[bank][:], fn, [], [pb[bank], dst_b])
                    w_release()
                for step in range(2 * NB):
                    b, hl = step // 2, step % 2
                    blk = slice(b * 128, (b + 1) * 128)
                    bsc = 0 if step < 4 else 4
                    cs = slice((step % 4) * 128, (step % 4 + 1) * 128)
                    for dc in range(2):
                        mm(ps[bsc][:, cs], kT[:, 2 * hl + dc, blk], qT[:, 2 * hl + dc, blk], dc == 0, dc == 1,
                           [kT_b, qT_b], bsc)
                for step in range(2 * NB):
                    hl = step % 2
                    hh = 2 * hp + hl
                    bsc = 0 if step < 4 else 4
                    cs = slice((step % 4) * 128, (step % 4 + 1) * 128)
                    scs, scb = sc[step]
                    tt(scs, ps[bsc][:, cs], C("maskT", hh * 128, 128), ALU.mult, [cst_b], [pb[bsc], scb])
                pend_T = None
                for b in range(NB):
                    blk = slice(b * 128, (b + 1) * 128)
                    for hl in range(2):
                        hh = 2 * hp + hl
                        step = 2 * b + hl
                        bo_, bst, bT = 4 * hl + 1, 4 * hl + 2, 4 * hl + 3
                        scs, scb = sc[step]
                        for dc in range(2):
                            mm(ps[bo_][:, 0:256], qx[:, 2 * hl + dc, blk], S_bf[:, hh, dc * 256:(dc + 1) * 256],
                               dc == 0, False, [qx_b, Sb_b[hh]], bo_, mark=False)
                        mm(ps[bo_][:, 0:256], scs, v_sb[:, b, hl * 256:(hl + 1) * 256], False, True, [scb, v_b], bo_)
                        for dc in range(2):
                            mm(ps[bst][:, dc * 256:(dc + 1) * 256], kz[:, b, hl * 256 + dc * 128: hl * 256 + (dc + 1) * 128],
                               v_sb[:, b, hl * 256:(hl + 1) * 256], True, True, [kz_b, v_b], bst, mark=(dc == 1))
                        if pend_T is not None:
                            pend_T()
                        gk = float((1.0 - 2.0 ** (-5 - hh)) ** 128)
                        stt(S_f32[:, j, hh, :], S_f32[:, j, hh, :], gk, ps[bst][:], ALU.mult, ALU.add,
                            [], [Sf_b[j][hh], pb[bst]])
                        P.op("act", lambda e, hh=hh: e.copy(out=S_bf[:, hh, :], in_=S_f32[:, j, hh, :]),
                             reads=[Sf_b[j][hh]], writes=[Sb_b[hh]])
                        sta, stb = stats[step % 4]
                        mva, mvb = mv[step % 4]
                        ona, onb = on[step % 4]
                        boa, bob = bo[step % 4]
                        P.op("dve", lambda e, sta=sta, bo_=bo_: e.bn_stats(sta[:, 0:6], ps[bo_][:, 0:256]),
                             reads=[], writes=[pb[bo_], stb])
                        P.op("dve", lambda e, sta=sta, mva=mva: e.bn_aggr(mva[:, 0:2], sta[:, 0:6]), reads=[stb], writes=[mvb])
                        act(mva[:, 2:3], mva[:, 1:2], AF.Sqrt, [], [mvb], bias=LN_EPS, scale=1.0)
                        recip(mva[:, 3:4], mva[:, 2:3], [], [mvb])
                        ts(ona, ps[bo_][:, 0:256], mva[:, 0:1], mva[:, 3:4], ALU.subtract, ALU.mult, [mvb], [pb[bo_], onb])
                        tt(boa, ona, sg_sb[:, b, hl * 256:(hl + 1) * 256], ALU.mult, [onb, sg_b], [bob])

                        def do_T(boa=boa, bob=bob, bT=bT, hh=hh, blk=blk):
                            pv = ps[bT][:, 0:128].bitcast(BF16)
                            for ec in range(2):
                                P.op("pe", lambda e, pv=pv, ec=ec, boa=boa: e.transpose(
                                    pv[:, ec * 128:(ec + 1) * 128], boa[:, ec * 128:(ec + 1) * 128], ident[:]),
                                    reads=[bob, misc_b], writes=[pb[bT]])
                            for ec in range(2):
                                cidx = 8 + 2 * hh + ec
                                act(cat[:, cidx, blk], pv[:, ec * 128:(ec + 1) * 128], AF.Identity, [cst_b],
                                    [pb[bT], catb[cidx]], scale=C("rlg", j * 8 + 2 * hh + ec))
                        pend_T = do_T
                pend_T()
            out_proj(li, cat, catb)

        def odd_mixer(li):
            j = li // 2
            ar = Arena()
            uT_f, _ = ar.bf16(KC * T)
            uT = uT_f.rearrange("p (k t) -> p k t", k=KC)
            ub = [Buf(ar.bar) for _ in range(KC)]
            vt_f, _ = ar.f32(NB * 2048)
            v_tok = vt_f.rearrange("p (b f) -> p b f", b=NB)
            vtb = [Buf(ar.bar) for _ in range(NB)]
            vn_f, _ = ar.bf16(NB * 2048)
            vn = vn_f.rearrange("p (b f) -> p b f", b=NB)
            vnb = [Buf(ar.bar) for _ in range(NB)]
            bias_g = [ar.f32(4 * 128) for _ in range(1)]
            tmp = [ar.f32(512) for _ in range(2)]
            stats, stats_b = ar.f32(24)
            mv, mv_b = ar.f32(4)
            for cg in range(4):
                slot, sbuf = w_acquire((li, "gv", cg))
                sv = slot[:].rearrange("p (k c) -> p k c", k=KC)
                for b in range(NB):
                    bank = nbank()
                    for kc in range(KC):
                        mm(ps[bank][:], h_sb[:, kc, b * 128:(b + 1) * 128], sv[:, kc, :], kc == 0, kc == KC - 1,
                           [sbuf[kc // 8], hb[kc]], bank)
                    act(v_tok[:, b, cg * 512:(cg + 1) * 512], ps[bank][:], AF.Gelu, [], [pb[bank], vtb[b]])
                w_release()
            for b in range(NB):
                for cg in range(4):
                    P.op("dve", lambda e, b=b, cg=cg: e.bn_stats(stats[:, cg * 6:(cg + 1) * 6], v_tok[:, b, cg * 512:(cg + 1) * 512]),
                         reads=[vtb[b]], writes=[stats_b])
                P.op("dve", lambda e: e.bn_aggr(mv[:, 0:2], stats[:, 0:24]), reads=[stats_b], writes=[mv_b])
                act(mv[:, 2:3], mv[:, 1:2], AF.Sqrt, [], [mv_b], bias=LN_EPS, scale=1.0)
                recip(mv[:, 3:4], mv[:, 2:3], [], [mv_b])
                ts(vn[:, b, :], v_tok[:, b, :], mv[:, 0:1], mv[:, 3:4], ALU.subtract, ALU.mult, [mv_b, vtb[b]], [vnb[b]])
            for mp in range(4):
                slot, sbuf = w_acquire((li, "gu", mp))
                sv = slot[:].rearrange("p (k c) -> p k c", k=KC)
                for m in range(4):
                    mo = mp * 4 + m
                    bank = nbank()
                    proj_fm(sv, sbuf, m, bank, lambda kc: h_sb[:, kc, :], lambda kc: [hb[kc]])
                    act(uT[:, mo, :], ps[bank][:], AF.Gelu, [], [pb[bank], ub[mo]])
                w_release()
                g = mp
                bg, bgb = bias_g[0]
                for fl in range(4):
                    fc = 4 * g + fl
                    stt(bg[:, fl * 128:(fl + 1) * 128], rowsum[:, j, g, :], C("glb", j * 16 + fc),
                        C("bsbc", (j * 4 + g) * 128, 128), ALU.mult, ALU.add, [misc_b, cst_b], [bgb])
                for b in range(NB):
                    blk = slice(b * 128, (b + 1) * 128)
                    bank = nbank()
                    for fl in range(4):
                        fc = 4 * g + fl
                        mm(ps[bank][:, fl * 128:(fl + 1) * 128], vn[:, b, fc * 128:(fc + 1) * 128], wmT[:, j, g, :], True, True,
                           [vnb[b], misc_b], bank, mark=(fl == 3))
                    tm, tmb = tmp[b % 2]
                    for fl in range(4):
                        fc = 4 * g + fl
                        act(tm[:, fl * 128:(fl + 1) * 128], ps[bank][:, fl * 128:(fl + 1) * 128], AF.Identity, [cst_b],
                            [pb[bank], tmb], scale=C("glg", j * 16 + fc))
                    tt(tm, tm, bg, ALU.add, [bgb], [tmb])
                    tt(uT[:, 4 * g:4 * g + 4, blk], tm.rearrange("p (f i) -> p f i", f=4), uT[:, 4 * g:4 * g + 4, blk],
                       ALU.mult, [tmb], [ub[4 * g + i] for i in range(4)])
            out_proj(li, uT, ub)

        def ffn(li):
            ar = Arena()
            g_f, _ = ar.bf16(44 * T)
            g_sb = g_f.rearrange("p (k t) -> p k t", k=44)
            gb = [Buf(ar.bar) for _ in range(44)]
            ya = [ar.f32(T) for _ in range(2)]
            yb = [ar.f32(T) for _ in range(2)]
            sa = [ar.f32(T) for _ in range(2)]
            corr_f, corr_b = ar.f32(176)
            corr = corr_f.rearrange("p (c t) -> p c t", t=2)
            t88, t88_b = ar.f32(88)
            w0 = C("fdw", (li * 3 + 0) * 88, 88)
            w1 = C("fdw", (li * 3 + 1) * 88, 88)
            tt(corr[:, :, 1], w0, fh[:, li, :, 1], ALU.mult, [cst_b, fh_b[li]], [corr_b])
            tt(t88, w0, fh[:, li, :, 0], ALU.mult, [cst_b, fh_b[li]], [t88_b])
            tt(corr[:, :, 0], w1, fh[:, li, :, 1], ALU.mult, [cst_b, fh_b[li]], [corr_b])
            tt(corr[:, :, 0], corr[:, :, 0], t88, ALU.add, [t88_b], [corr_b])
            for i in range(22):
                slot, sbuf = w_acquire((li, "up", i))
                sv = slot[:].rearrange("p (k c) -> p k c", k=KC)
                for cc in range(2):
                    ci = 2 * i + cc
                    bA, bB = nbank(), nbank()
                    proj_fm(sv, sbuf, 2 * cc, bA, lambda kc: h_sb[:, kc, :], lambda kc: [hb[kc]])
                    proj_fm(sv, sbuf, 2 * cc + 1, bB, lambda kc: h_sb[:, kc, :], lambda kc: [hb[kc]])
                    ys = (ya[ci % 2], yb[ci % 2])
                    for (bank, ch, (y, y_b)) in ((bA, ci, ys[0]), (bB, 44 + ci, ys[1])):
                        act(y, ps[bank][:], AF.Identity, [cst_b], [pb[bank], y_b],
                            bias=C("fdb", li * 88 + ch), scale=C("fdw", (li * 3 + 2) * 88 + ch))
                        P.op("act", lambda e, bank=bank, ch=ch: e.copy(out=fh[:, li, ch, :], in_=ps[bank][:, T - 2:T]),
                             reads=[corr_b], writes=[pb[bank], fh_b[li]])
                        stt(y[:, 1:T], ps[bank][:, 0:T - 1], C("fdw", (li * 3 + 1) * 88 + ch), y[:, 1:T], ALU.mult, ALU.add,
                            [], [pb[bank], y_b])
                        stt(y[:, 2:T], ps[bank][:, 0:T - 2], C("fdw", (li * 3 + 0) * 88 + ch), y[:, 2:T], ALU.mult, ALU.add,
                            [], [pb[bank], y_b])
                        tt(y[:, 0:2], y[:, 0:2], corr[:, ch, :], ALU.add, [corr_b], [y_b])
                    s, s_b = sa[ci % 2]
                    act(s, ys[0][0], AF.Silu, [ys[0][1]], [s_b])
                    tt(g_sb[:, ci, :], s, ys[1][0], ALU.mult, [s_b, ys[1][1]], [gb[ci]])
                w_release()
            for m in range(16):
                slot, sbuf = w_acquire((li, "down", m))
                bank = nbank()
                for kc in range(44):
                    mm(ps[bank][:], slot[:, kc * 128:(kc + 1) * 128], g_sb[:, kc, :], kc == 0, kc == 43, [sbuf[kc // 22], gb[kc]], bank)
                tt(x_sb[:, m, :], x_sb[:, m, :], ps[bank][:], ALU.add, [], [xb[m], pb[bank]])
                w_release()

        last_store = None
        for ti in range(n_tiles):
            seq_first = (ti % TPS == 0)
            tok0 = ti * T
            pos0 = (ti % TPS) * T
            for qd in range(4):
                P.op("sp", lambda e, tok0=tok0, qd=qd: e.dma_start(out=x_sb[:, 4 * qd:4 * qd + 4, :],
                                                                 in_=xT_v[:, 4 * qd:4 * qd + 4, tok0:tok0 + T]),
                     writes=xb[4 * qd:4 * qd + 4], sem=x_sems[qd], inc=16)
            P.op("sp", lambda e, pos0=pos0: e.dma_start(out=cos_sb[:], in_=rope_d[0, :, pos0:pos0 + T]),
                 writes=[cos_b], sem=cos_sem, inc=16)
            P.op("sp", lambda e, pos0=pos0: e.dma_start(out=sin_sb[:], in_=rope_d[1, :, pos0:pos0 + T]),
                 writes=[sin_b], sem=sin_sem, inc=16)
            if seq_first:
                P.op("dve", lambda e: e.memset(S_f32[:], 0.0), writes=[b for r in Sf_b for b in r])
                P.op("dve", lambda e: e.memset(S_bf[:], 0.0), writes=Sb_b)
                P.op("dve", lambda e: e.memset(uh[:], 0.0), writes=[b for r in uh_b for b in r])
                P.op("dve", lambda e: e.memset(fh[:], 0.0), writes=fh_b)
            for li in layers:
                rmsnorm("mixg", li)
                if li % 2 == 0:
                    if not seq_first:
                        for hh in range(4):
                            P.op("act", lambda e, hh=hh, li=li: e.copy(out=S_bf[:, hh, :], in_=S_f32[:, li // 2, hh, :]),
                                 reads=[Sf_b[li // 2][hh]], writes=[Sb_b[hh]])
                    else:
                        P.op("dve", lambda e: e.memset(S_bf[:], 0.0), writes=Sb_b)
                    even_mixer(li)
                else:
                    odd_mixer(li)
                if do_ffn:
                    rmsnorm("ffng", li)
                    ffn(li)
            if final_norm:
                stg, stg_b = rmsnorm("fing", 0, in_place=True)
                last_store = P.op("sp", lambda e, tok0=tok0, stg=stg: e.dma_start(
                    out=outT_v[:, :, tok0:tok0 + T], in_=stg.rearrange("p (k t) -> p k t", k=KC)),
                    reads=[stg_b], sem=o_sem, inc=16)
                extra_bar[:] = [last_store]
            else:
                last_store = P.op("sp", lambda e, tok0=tok0: e.dma_start(out=outT_v[:, :, tok0:tok0 + T], in_=x_sb[:]),
                                  reads=xb, sem=o_sem, inc=16)
        P.wait_all("sp", [last_store, cos_b.last_write, sin_b.last_write, cst_b.last_write])
        P.finish()
        build_program.stats = (P.n_ops, P.n_wait)
    return nc


def _pm(v, c):
    v = np.asarray(v, np.float32)
    lead = v.shape[:-1]
    return np.moveaxis(v.reshape(lead + (c, 128)), -1, 0)


def _col_piece(W, cols):
    blk = W[:, cols]
    return blk.reshape(16, 128, 512).transpose(1, 0, 2).reshape(128, 8192)


def _consts():
    H = 4
    hh = np.arange(H, dtype=np.float64)
    gamma = 1.0 - 2.0 ** (-5.0 - hh)
    pos = np.arange(128, dtype=np.float64)
    n = pos[None, :]
    m = pos[:, None]
    cm = (m // 64)
    cn = (n // 64)
    maskT = np.zeros((128, H, 128))
    for h in range(H):
        same = gamma[h] ** np.abs(n - m)
        caus = gamma[h] ** (n - m)
        maskT[:, h, :] = np.where(cm == cn, same, np.where(cn > cm, caus, 0.0)) / 16.0
    zeta = np.stack([gamma[h] ** (127.0 - pos) / 16.0 for h in range(H)], axis=1)
    xi = np.broadcast_to(np.stack([gamma[h] ** (pos + 1.0) for h in range(H)], axis=0)[None], (128, H, 128))
    ci = np.arange(128) // 64
    wmask = (ci[:, None] <= ci[None, :]).astype(np.float32)
    inv = (10000.0 ** (-np.arange(0, 256, 2, dtype=np.float32) / 256.0)).astype(np.float32)
    ang = (np.arange(SEQ, dtype=np.float32)[:, None] * inv[None, :]).astype(np.float32)
    rope = np.stack([np.cos(ang).T, np.sin(ang).T]).astype(np.float32)
    return maskT.astype(np.float32), zeta.astype(np.float32), np.ascontiguousarray(xi, dtype=np.float32), wmask, rope


def prep_inputs(inp):
    f = lambda k: np.asarray(inp[k], np.float32)
    maskT, zeta, xi, wmask, rope = _consts()
    cst = np.zeros((128, NCST), np.float32)

    def put(name, arr):
        a = np.ascontiguousarray(arr, dtype=np.float32).reshape(128, -1)
        o = CST_OFF[name]
        cst[:, o:o + a.shape[1]] = a

    put("mixg", _pm(f("mix_norm_g"), 16))
    put("ffng", _pm(f("ffn_norm_g"), 16))
    put("fing", _pm(f("final_norm_g"), 16))
    put("fdw", _pm(f("ffn_dw_w"), 88))
    put("fdb", _pm(f("ffn_dw_b"), 88))
    put("cdw", _pm(f("ev_conv_dw_w"), 8))
    put("cdb", _pm(f("ev_conv_dw_b"), 8))
    put("clg", _pm(f("ev_conv_ln_g"), 8))
    put("clb", _pm(f("ev_conv_ln_b"), 8))
    put("rlg", _pm(f("ev_ret_ln_g"), 8))
    put("glg", _pm(f("od_gm_ln_g"), 16))
    put("glb", _pm(f("od_gm_ln_b"), 16))
    put("zeta", zeta)
    put("maskT", maskT)
    put("xi", xi)
    put("bsbc", np.broadcast_to(f("od_gm_bs")[None], (128, 2, 4, 128)))
    put("wmask", wmask)
    put("ident", np.eye(128, dtype=np.float32))
    wsT = np.ascontiguousarray(np.transpose(f("od_gm_ws"), (3, 0, 1, 2))).reshape(128, 1024)

    wstream = np.zeros((NPIECE, 128, 8192), np.float32)
    ar = np.arange
    for li in range(4):
        j = li // 2
        base = PIECE_BASE[li]
        for pi, kd in enumerate(layer_pieces(li)):
            dst = wstream[base + pi]
            k0, k1 = kd
            if k0 == "glu":
                W = f("ev_w_in")[j]
                c0 = k1 * 256
                dst[:] = _col_piece(W, np.concatenate([ar(c0, c0 + 128), 1024 + ar(c0, c0 + 128),
                                                       ar(c0 + 128, c0 + 256), 1024 + ar(c0 + 128, c0 + 256)]))
            elif k0 in ("q", "k", "v", "gate"):
                W = f("ev_w_in")[j]
                off = {"q": 2048, "k": 3072, "v": 4096, "gate": 5120}[k0]
                dst[:] = _col_piece(W, off + ar(k1 * 512, (k1 + 1) * 512))
            elif k0 == "mout":
                W = f("ev_w_out")[j] if li % 2 == 0 else f("od_w_out")[j]
                dst[:] = _col_piece(W, ar(k1 * 512, (k1 + 1) * 512))
            elif k0 == "gu":
                dst[:] = _col_piece(f("od_w_in")[j], ar(k1 * 512, (k1 + 1) * 512))
            elif k0 == "gv":
                dst[:] = _col_piece(f("od_w_in")[j], 2048 + ar(k1 * 512, (k1 + 1) * 512))
            elif k0 == "up":
                W = f("ffn_w_up")[li]
                c0 = k1 * 256
                dst[:] = _col_piece(W, np.concatenate([ar(c0, c0 + 128), DFF + ar(c0, c0 + 128),
                                                       ar(c0 + 128, c0 + 256), DFF + ar(c0 + 128, c0 + 256)]))
            elif k0 == "down":
                Wd = f("ffn_w_down")[li]
                blk = Wd[:, k1 * 128:(k1 + 1) * 128]
                dst[:, :5632] = blk.reshape(44, 128, 128).transpose(1, 0, 2).reshape(128, 5632)
    return {"wstream": wstream, "cst": cst, "wsT": wsT, "rope": rope}


_PROG_CACHE = {}


def kernel(**inputs):
    x = np.asarray(inputs["x"], np.float32)
    shared = prep_inputs(inputs)
    if "nc" not in _PROG_CACHE:
        _PROG_CACHE["nc"] = build_program()
    nc = _PROG_CACHE["nc"]
    in_maps = []
    for c in range(NCORE):
        xc = x[c * NSEQ:(c + 1) * NSEQ].reshape(NSEQ * SEQ, D)
        m = dict(shared)
        m["xT"] = np.ascontiguousarray(xc.T)
        in_maps.append(m)
    res = run_bass_kernel_spmd(nc, in_maps, core_ids=list(range(NCORE)))
    out = np.empty((NCORE * NSEQ, SEQ, D), np.float32)
    for c in range(NCORE):
        out[c * NSEQ:(c + 1) * NSEQ] = np.asarray(res.results[c]["outT"]).T.reshape(NSEQ, SEQ, D)
    return out
```
